# Optimizing a Trainium2 kernel written in Bass

```python
import math
import jax, jax.numpy as jnp
from jax import lax
import numpy as np

D_MODEL = 1024
BATCH = 32
SEQ = 2048
DEPTH = 4

GRID_W = 64
CTX_LEN = 256
N_MOD = 6
ATTN_HEADS = 4
ATTN_QK_DIM = 64
ATTN_V_DIM = 2 * ATTN_QK_DIM
Q_COLS = ATTN_HEADS * 2 * ATTN_QK_DIM
K_COLS = Q_COLS
V_COLS = ATTN_HEADS * ATTN_V_DIM
ATTN_SCALE = ATTN_QK_DIM ** -0.5
ROPE_THETA = 10000.0
BLOCK_Q = 128
CONV_WIDTH = D_MODEL // 4
POOL_WINDOWS = (2, 4, 8, 16)
POOL_GROUPS = len(POOL_WINDOWS)
POOL_WIDTH = D_MODEL - V_COLS - CONV_WIDTH
POOL_GROUP_DIM = POOL_WIDTH // POOL_GROUPS
Q_END = Q_COLS
K_END = Q_END + K_COLS
V_END = K_END + V_COLS
CONV_END = V_END + 3 * CONV_WIDTH
IN_COLS = CONV_END + POOL_WIDTH
MIX_WIDTH = V_COLS + CONV_WIDTH + POOL_WIDTH
FFN_HIDDEN = -(-(8 * D_MODEL) // (3 * 256)) * 256
EPS = 1e-6

kernel_name = "hybrid_diff_conv_pool_dit"


def rms_norm(x, g):
    xf = x.astype(jnp.float32)
    y = xf * lax.rsqrt(jnp.mean(xf * xf, axis=-1, keepdims=True) + EPS)
    return y.astype(x.dtype) * g


def modulate(h, shift, scale):
    return h * (1.0 + scale) + shift


def axial_rope_tables(length, dtype):
    rows = length // GRID_W
    row = jnp.broadcast_to(jnp.arange(rows)[:, None], (rows, GRID_W)).reshape(-1).astype(jnp.float32)
    col = jnp.broadcast_to(jnp.arange(GRID_W)[None, :], (rows, GRID_W)).reshape(-1).astype(jnp.float32)
    nf = ATTN_QK_DIM // 4
    inv = 1.0 / (ROPE_THETA ** (jnp.arange(nf, dtype=jnp.float32) / nf))
    ang_r = row[:, None] * inv[None, :]
    ang_c = col[:, None] * inv[None, :]
    ang = jnp.concatenate([ang_r, ang_r, ang_c, ang_c], axis=-1)
    return jnp.cos(ang).astype(dtype), jnp.sin(ang).astype(dtype)


def apply_rope(t, cos, sin):
    nf = ATTN_QK_DIM // 4
    tr = t.reshape(t.shape[:-1] + (2, 2, nf))
    rot = jnp.stack([-tr[..., 1, :], tr[..., 0, :]], axis=-2).reshape(t.shape)
    return t * cos + rot * sin


def split_heads_qk(t):
    b, l, _ = t.shape
    return t.reshape(b, l, ATTN_HEADS, 2, ATTN_QK_DIM).transpose(0, 2, 3, 1, 4)


def split_heads_v(t):
    b, l, _ = t.shape
    return t.reshape(b, l, ATTN_HEADS, ATTN_V_DIM).transpose(0, 2, 1, 3)


def diff_attention_core(q, k, v, lam):
    s = jnp.einsum('bhcqd,bhckd->bhcqk', q, k).astype(jnp.float32) * ATTN_SCALE
    p = jax.nn.softmax(s, axis=-1)
    a = p[:, :, 0] - lam * p[:, :, 1]
    return jnp.einsum('bhqk,bhkd->bhqd', a.astype(v.dtype), v)


def diff_attn_post(o, g_sub, lam_init):
    o = rms_norm(o, g_sub) * (1.0 - lam_init)
    b, h, l, dv = o.shape
    return o.transpose(0, 2, 1, 3).reshape(b, l, h * dv)


def conv_mixer(bcx, w_conv):
    bg, cg, xin = jnp.split(bcx, 3, axis=-1)
    u = cg * xin
    up = jnp.pad(u, ((0, 0), (1, 1), (0, 0)))
    y = up[:, :-2] * w_conv[0] + up[:, 1:-1] * w_conv[1] + up[:, 2:] * w_conv[2]
    return bg * y


def pool_mixer(p, w_pool, s_pool):
    b, l, _ = p.shape
    pf = p.astype(jnp.float32)
    cs = jnp.concatenate([jnp.zeros((b, 1, POOL_WIDTH), jnp.float32), jnp.cumsum(pf, axis=1)], axis=1)
    cs = cs.reshape(b, l + 1, POOL_GROUPS, POOL_GROUP_DIM)
    t = jnp.arange(l)[:, None]
    win = jnp.array(POOL_WINDOWS, dtype=jnp.int32)[None, :]
    lo = jnp.clip(t - win // 2, 0, l - 1)
    hi = jnp.clip(t + win - 1 - win // 2, 0, l - 1)
    grp = jnp.arange(POOL_GROUPS)[None, :]
    cnt = (hi - lo + 1).astype(jnp.float32)[None, :, :, None]
    mean = (cs[:, hi + 1, grp, :] - cs[:, lo, grp, :]) / cnt
    pooled = (mean - pf.reshape(b, l, POOL_GROUPS, POOL_GROUP_DIM)).astype(p.dtype)
    y = jnp.einsum('blgc,gcd->blgd', pooled, w_pool).reshape(b, l, POOL_WIDTH)
    return y * s_pool


def swiglu(h, w_gate_up, w_down):
    g, u = jnp.split(h @ w_gate_up, 2, axis=-1)
    return (jax.nn.silu(g) * u) @ w_down


def setup_inputs(seed: int = 0) -> dict:
    key = jax.random.key(seed)
    ks = jax.random.split(key, 24)
    f32 = jnp.float32
    D = D_MODEL

    def nrm(k, shape, scale):
        return jax.random.normal(k, shape, f32) * scale

    return {
        'x': nrm(ks[0], (BATCH, SEQ, D), 1.0),
        'c': nrm(ks[1], (BATCH, D), 1.0),
        'ctx': nrm(ks[2], (BATCH, CTX_LEN, D), 1.0),
        'c_ctx': nrm(ks[3], (D,), 1.0),
        'w_ada': nrm(ks[4], (DEPTH, D, N_MOD * D), 0.5 * D ** -0.5),
        'b_ada': nrm(ks[5], (DEPTH, N_MOD * D), 0.02),
        'g_norm1': 1.0 + nrm(ks[6], (DEPTH, D), 0.05),
        'w_in': nrm(ks[7], (DEPTH, D, IN_COLS), D ** -0.5),
        'lam_q1': nrm(ks[8], (DEPTH, ATTN_QK_DIM), 0.1),
        'lam_k1': nrm(ks[9], (DEPTH, ATTN_QK_DIM), 0.1),
        'lam_q2': nrm(ks[10], (DEPTH, ATTN_QK_DIM), 0.1),
        'lam_k2': nrm(ks[11], (DEPTH, ATTN_QK_DIM), 0.1),
        'g_subln': 1.0 + nrm(ks[12], (DEPTH, ATTN_V_DIM), 0.05),
        'w_conv': nrm(ks[13], (DEPTH, 3, CONV_WIDTH), 3 ** -0.5),
        'w_pool': nrm(ks[14], (DEPTH, POOL_GROUPS, POOL_GROUP_DIM, POOL_GROUP_DIM), POOL_GROUP_DIM ** -0.5),
        's_pool': 1.0 + nrm(ks[15], (DEPTH, POOL_WIDTH), 0.1),
        'w_out': nrm(ks[16], (DEPTH, MIX_WIDTH, D), MIX_WIDTH ** -0.5),
        'g_norm2': 1.0 + nrm(ks[17], (DEPTH, D), 0.05),
        'w_gate_up': nrm(ks[18], (DEPTH, D, 2 * FFN_HIDDEN), D ** -0.5),
        'w_down': nrm(ks[19], (DEPTH, FFN_HIDDEN, D), FFN_HIDDEN ** -0.5),
        'g_final': 1.0 + nrm(ks[20], (D,), 0.05),
    }


def reference(x, c, ctx, c_ctx, w_ada, b_ada, g_norm1, w_in, lam_q1, lam_k1, lam_q2, lam_k2,
              g_subln, w_conv, w_pool, s_pool, w_out, g_norm2, w_gate_up, w_down, g_final):
    b, seq_len, _ = x.shape
    n_blocks = seq_len // BLOCK_Q
    cos, sin = axial_rope_tables(seq_len, x.dtype)
    s_lat = jax.nn.silu(c)
    s_ctx = jax.nn.silu(c_ctx)
    x_lat, x_ctx = x, ctx
    for l in range(DEPTH):
        last = l == DEPTH - 1
        lam_init = 0.8 - 0.6 * math.exp(-0.3 * l)
        lam = (jnp.exp(jnp.sum(lam_q1[l] * lam_k1[l]).astype(jnp.float32))
               - jnp.exp(jnp.sum(lam_q2[l] * lam_k2[l]).astype(jnp.float32)) + lam_init)

        mod = s_lat @ w_ada[l] + b_ada[l]
        sh1, sc1, gt1, sh2, sc2, gt2 = jnp.split(mod[:, None, :], N_MOD, axis=-1)
        n_ctx_mod = 2 if last else N_MOD
        mod_c = s_ctx @ w_ada[l][:, :n_ctx_mod * D_MODEL] + b_ada[l][:n_ctx_mod * D_MODEL]
        mods_c = jnp.split(mod_c, n_ctx_mod)

        h = modulate(rms_norm(x_lat, g_norm1[l]), sh1, sc1)
        hc = modulate(rms_norm(x_ctx, g_norm1[l]), mods_c[0], mods_c[1])

        proj = h @ w_in[l]
        kv_c = hc @ w_in[l][:, Q_END:V_END]
        k_c = split_heads_qk(kv_c[..., :K_COLS])
        v_c = split_heads_v(kv_c[..., K_COLS:])

        q_l = apply_rope(split_heads_qk(proj[..., :Q_END]), cos, sin)
        k_l = apply_rope(split_heads_qk(proj[..., Q_END:K_END]), cos, sin)
        v_l = split_heads_v(proj[..., K_END:V_END])
        k_all = jnp.concatenate([k_c, k_l], axis=3)
        v_all = jnp.concatenate([v_c, v_l], axis=2)
        qb = jnp.moveaxis(q_l.reshape(b, ATTN_HEADS, 2, n_blocks, BLOCK_Q, ATTN_QK_DIM), 3, 0)
        o_l = lax.map(lambda qi: diff_attention_core(qi, k_all, v_all, lam), qb)
        o_l = jnp.moveaxis(o_l, 0, 2).reshape(b, ATTN_HEADS, seq_len, ATTN_V_DIM)

        mix = jnp.concatenate([
            diff_attn_post(o_l, g_subln[l], lam_init),
            conv_mixer(proj[..., V_END:CONV_END], w_conv[l]),
            pool_mixer(proj[..., CONV_END:], w_pool[l], s_pool[l]),
        ], axis=-1) @ w_out[l]
        x_lat = x_lat + gt1 * mix
        h2 = modulate(rms_norm(x_lat, g_norm2[l]), sh2, sc2)
        x_lat = x_lat + gt2 * swiglu(h2, w_gate_up[l], w_down[l])

        if not last:
            q_c = split_heads_qk(hc @ w_in[l][:, :Q_END])
            rest_c = hc @ w_in[l][:, V_END:]
            o_c = diff_attention_core(q_c, k_c, v_c, lam)
            mix_c = jnp.concatenate([
                diff_attn_post(o_c, g_subln[l], lam_init),
                conv_mixer(rest_c[..., :3 * CONV_WIDTH], w_conv[l]),
                pool_mixer(rest_c[..., 3 * CONV_WIDTH:], w_pool[l], s_pool[l]),
            ], axis=-1) @ w_out[l]
            x_ctx = x_ctx + mods_c[2] * mix_c
            h2c = modulate(rms_norm(x_ctx, g_norm2[l]), mods_c[3], mods_c[4])
            x_ctx = x_ctx + mods_c[5] * swiglu(h2c, w_gate_up[l], w_down[l])
    return rms_norm(x_lat, g_final)
```

```python
import math
from contextlib import ExitStack

import ml_dtypes
import numpy as np

import concourse.bass as bass
import concourse.mybir as mybir
from concourse.bass_utils import run_bass_kernel_spmd

F32 = mybir.dt.float32
BF16 = mybir.dt.bfloat16
AF = mybir.ActivationFunctionType
ALU = mybir.AluOpType
AX = mybir.AxisListType

D = 1024
L = 2048
CL = 256
NT = L + CL
DEPTH = 4
BATCH = 32
NCORES = 8
KC = 8
FFN = 2816
NJ = FFN // 128
EPS = 1e-6
TT = [(0, 512), (512, 1024), (1024, 1536), (1536, 2048), (2048, 2304)]
PADW = 2336
NVEC = 73
WS_ELEMS = 2048
NSLOT = 3
ARENA_BYTES = 70144


def padcol(t):
    return 8 + t if t < L else 2072 + (t - L)


class Counter:
    LIMIT = 4000

    def __init__(self, kb, name):
        self.kb = kb
        self.name = name
        self.epoch = 0
        self.val = 0
        self.sems = [kb.new_sem(f"{name}_0")]

    def bump(self, inc):
        if self.val + inc > self.LIMIT:
            self.epoch += 1
            self.val = 0
            self.sems.append(self.kb.new_sem(f"{self.name}_{self.epoch}"))
        self.val += inc
        return self.sems[self.epoch], (self, self.epoch, self.val)

    def cur(self):
        return (self, self.epoch, self.val)


class Buf:
    __slots__ = ("w", "r", "name", "excl")

    def __init__(self, name, init_r=None, excl=False):
        self.name = name
        self.excl = excl
        self.w = None
        self.r = dict(init_r) if init_r else {}


class Eng:
    def __init__(self, kb, name, eng, is_pe=False, is_queue=False):
        self.kb = kb
        self.name = name
        self.eng = eng
        self.is_pe = is_pe
        self.is_queue = is_queue
        self.counter = None if is_queue else Counter(kb, name)
        self.seen = {}

    def wait_for(self, ev):
        if ev is None:
            return
        C, e, v = ev
        if C is self.counter:
            if self.is_pe:
                return
            if e == C.epoch and C.val - v >= 2:
                return
        s = self.seen.get(C)
        if s is not None and s >= (e, v):
            return
        if v <= 0:
            return
        self.eng.wait_ge(C.sems[e], v)
        self.seen[C] = (e, v)


class KB:
    def __init__(self, nseq, depth):
        self.nseq = nseq
        self.depth = depth
        self.nc = bass.Bass("TRN2", target_bir_lowering=False)
        self.es = ExitStack()
        self.nsem = 0

    def new_sem(self, name):
        self.nsem += 1
        return self.es.enter_context(self.nc.semaphore(f"s{self.nsem}_{name}"))

    def sb(self, name, shape, dt):
        return self.es.enter_context(self.nc.sbuf_tensor(name, shape, dt))

    def dram_in(self, name, shape, dt=F32):
        return self.nc.dram_tensor(name, list(shape), dt, kind="ExternalInput").ap()

    def _deps(self, reads, writes):
        deps = {}

        def add(ev):
            if ev is None:
                return
            C = ev[0]
            o = deps.get(C)
            if o is None or (ev[1], ev[2]) > (o[1], o[2]):
                deps[C] = ev

        for b in reads:
            add(b.w)
        for b in writes:
            add(b.w)
            for ev in b.r.values():
                add(ev)
        return deps

    def op(self, E, fn, reads=(), writes=()):
        ex = [b for b in reads if b.excl]
        if ex:
            writes = list(writes) + ex
        for ev in self._deps(reads, writes).values():
            E.wait_for(ev)
        ins = fn()
        sem, ev = E.counter.bump(1)
        ins.then_inc(sem, 1)
        for b in reads:
            b.r[E.counter] = ev
        for b in writes:
            b.w = ev
            b.r = {}
        return ev

    def dma(self, Q, out, in_, counter, reads=(), writes=(), nochain=False):
        deps = self._deps(reads, writes)
        if nochain:
            deps.pop(counter, None)
        for ev in deps.values():
            Q.wait_for(ev)
        ins = Q.eng.dma_start(out=out, in_=in_)
        sem, ev = counter.bump(16)
        ins.then_inc(sem, 16)
        for b in reads:
            b.r[counter] = ev
        for b in writes:
            b.w = ev
            b.r = {}
        return ev

    def new_phase(self):
        snap = {}
        for E in (self.PE, self.ACT, self.DVE, self.POOL):
            snap[E.counter] = E.counter.cur()
        for ev in self.arena_dma_evs:
            snap[ev[0]] = ev
        self.arena_dma_evs = []
        self.phase_snap = snap
        self.arena_off = 0

    def abuf(self, name):
        return Buf(name, init_r=self.phase_snap)

    def aalloc(self, nelem, dt):
        nb = nelem * (4 if dt == F32 else 2)
        nb = (nb + 63) // 64 * 64
        off = self.arena_off
        assert off + nb <= ARENA_BYTES, (off, nb, "arena overflow")
        self.arena_off += nb
        ap = self.arena[:, off // 2:(off + nb) // 2]
        if dt == F32:
            ap = ap.bitcast(F32)
        return ap[:, 0:nelem]

    def psA(self):
        i = self.ringA_i
        self.ringA_i = (i + 1) % 4
        return self.ps[i], self.psb[i]

    def psA2(self):
        i = self.ringA_i
        if i % 2:
            i = (i + 1) % 4
        self.ringA_i = (i + 2) % 4
        return self.psall[:, i:i + 2, :], [self.psb[i], self.psb[i + 1]]

    def psB(self):
        i = self.ringB_i
        self.ringB_i = (i + 1) % 4
        return self.ps[4 + i], self.psb[4 + i]

    def wnext(self, l, key):
        pl, pkey, n = self.plan[self.wpos]
        assert (pl, pkey) == (l, key), ((pl, pkey), (l, key), self.wpos)
        slot = self.wpos % NSLOT
        self.wpos += 1
        return self.wslot[slot], self.wslot_buf[slot]

    def wissue(self):
        i = self.wissued
        if i >= len(self.plan):
            return
        l, key, n = self.plan[i]
        slot = i % NSLOT
        ti = self.tile_index[key]
        self.dma(self.SP, self.wslot[slot][:, 0:n], self.wsc[l][ti, :, 0:n], self.wslot_cnt[slot],
                 reads=[self.wsc_buf[l]], writes=[self.wslot_buf[slot]])
        self.wissued += 1

    def wdone(self):
        self.wissue()


def layer_tiles():
    tiles = [("cgx", 0), ("cgx", 1), ("bg",), ("pool",), ("wo", 0, 0), ("wo", 0, 1)]
    for h in range(4):
        tiles += [("qk", h), ("v", h)]
    tiles += [("wo", 1, 0), ("wo", 1, 1)]
    tiles += [("gu", j) for j in range(NJ)]
    tiles += [("dn", half, jt) for half in range(2) for jt in range(6)]
    return tiles


def tile_nelems(key):
    if key[0] == "v":
        return 1024
    if key[0] == "dn" and key[2] == 5:
        return 1024
    return 2048


def ffn_token_tiles(last):
    t = [[(0, 512), (512, 1024)], [(1024, 1536), (1536, 2048)]]
    if not last:
        t.append([(2048, 2304)])
    return t


def make_plan(nseq, depth):
    plan = []
    for s in range(nseq):
        for l in range(depth):
            last = l == depth - 1
            for key in [("cgx", 0), ("cgx", 1), ("bg",), ("pool",), ("wo", 0, 0), ("wo", 0, 1)]:
                plan.append((l, key, tile_nelems(key)))
            for h in range(4):
                plan.append((l, ("qk", h), 2048))
                plan.append((l, ("v", h), 1024))
            plan.append((l, ("wo", 1, 0), 2048))
            plan.append((l, ("wo", 1, 1), 2048))
            for tok in ffn_token_tiles(last):
                for j in range(NJ):
                    plan.append((l, ("gu", j), 2048))
                for sub in tok:
                    for half in range(2):
                        for jt in range(6):
                            plan.append((l, ("dn", half, jt), tile_nelems(("dn", half, jt))))
    return plan


class StopBuild(Exception):
    pass


STOP_AT = None
CAST_ONLY = None


def ckpt(kb, name):
    if STOP_AT == name:
        for E in (kb.PE, kb.ACT, kb.DVE):
            kb.PQ.wait_for(E.counter.cur())
        raise StopBuild()


def build_program(nseq, depth):
    kb = KB(nseq, depth)
    nc = kb.nc
    with kb.es:
        try:
            _build(kb, nseq, depth)
        except StopBuild:
            pass
    return nc


def _build(kb, nseq, depth):
    nc = kb.nc
    x_d = kb.dram_in("x", [nseq, L, D])
    ctx_d = kb.dram_in("ctx", [nseq, CL, D])
    c_d = kb.dram_in("c", [nseq, D])
    cctx_d = kb.dram_in("c_ctx", [D])
    w_ada_d = kb.dram_in("w_ada", [DEPTH, D, 6 * D])
    b_ada_d = kb.dram_in("b_ada", [DEPTH, 6 * D])
    g1_d = kb.dram_in("g_norm1", [DEPTH, D])
    w_in_d = kb.dram_in("w_in", [DEPTH, D, 2560])
    lam_d = [kb.dram_in(n, [DEPTH, 64]) for n in ("lam_q1", "lam_k1", "lam_q2", "lam_k2")]
    gsub_d = kb.dram_in("g_subln", [DEPTH, 128])
    wconv_d = kb.dram_in("w_conv", [DEPTH, 3, 256])
    wpool_d = kb.dram_in("w_pool", [DEPTH, 4, 64, 64])
    spool_d = kb.dram_in("s_pool", [DEPTH, 256])
    w_out_d = kb.dram_in("w_out", [DEPTH, D, D])
    g2_d = kb.dram_in("g_norm2", [DEPTH, D])
    w_gu_d = kb.dram_in("w_gate_up", [DEPTH, D, 2 * FFN])
    w_dn_d = kb.dram_in("w_down", [DEPTH, FFN, D])
    gfin_d = kb.dram_in("g_final", [D])
    ident_d = kb.dram_in("k_ident", [128, 128])
    cos_d = kb.dram_in("k_cos", [128, L], BF16)
    sin_d = kb.dram_in("k_sin", [128, L], BF16)
    perm_d = kb.dram_in("k_perm", [128, 128], BF16)
    corr_d = kb.dram_in("k_corr", [128, 32])
    out_d = nc.dram_tensor("out", [nseq, L, D], F32, kind="ExternalOutput").ap()

    tiles = layer_tiles()
    kb.tile_index = {k: i for i, k in enumerate(tiles)}
    ntiles = len(tiles)
    kb.wsc = [nc.dram_tensor(f"wsc{l}", [ntiles, 128, WS_ELEMS], BF16, kind="Internal").ap()
              for l in range(depth)]
    kb.wsc_buf = [Buf(f"wsc{l}") for l in range(depth)]

    kb.PE = Eng(kb, "pe", nc.tensor, is_pe=True)
    kb.ACT = Eng(kb, "act", nc.scalar)
    kb.DVE = Eng(kb, "dve", nc.vector)
    kb.SP = Eng(kb, "sp", nc.sync, is_queue=True)
    kb.POOL = Eng(kb, "pool", nc.gpsimd)
    kb.PQ = kb.POOL
    PE, ACT, DVE, SP, PQ, POOL = kb.PE, kb.ACT, kb.DVE, kb.SP, kb.PQ, kb.POOL
    op, dma = kb.op, kb.dma

    xT = kb.sb("xT", [128, KC, NT], F32)
    hT = kb.sb("hT", [128, KC, NT], BF16)
    xT_b = [[Buf(f"xT{k}_{t}") for t in range(5)] for k in range(KC)]
    hT_b = [[Buf(f"hT{k}_{t}") for t in range(5)] for k in range(KC)]
    kb.arena = kb.sb("arena", [128, ARENA_BYTES // 2], BF16)
    wslots = kb.sb("wslots", [128, NSLOT, WS_ELEMS], BF16)
    kb.wslot = [wslots[:, i, :] for i in range(NSLOT)]
    kb.wslot_buf = [Buf(f"wslot{i}") for i in range(NSLOT)]
    kb.wslot_cnt = [Counter(kb, f"wsl{i}") for i in range(NSLOT)]
    cosT = kb.sb("cosT", [128, L], BF16)
    sinT = kb.sb("sinT", [128, L], BF16)
    ident = kb.sb("ident", [128, 128], F32)
    perm = kb.sb("perm", [128, 128], BF16)
    ones_bf = kb.sb("ones_bf", [128, 128], BF16)
    onesD = kb.sb("onesD", [128, 128], BF16)
    ones128 = kb.sb("ones128", [128, 128], BF16)
    ones_f32 = kb.sb("ones_f32", [128, 128], F32)
    VT = kb.sb("VT", [128, DEPTH, NVEC], F32)
    modT = kb.sb("modT", [128, DEPTH, 48, 5], F32)
    AB = kb.sb("AB", [128, DEPTH, 2, KC, 5], F32)
    neglam = kb.sb("neglam", [128, DEPTH], F32)
    gs = kb.sb("gs", [128, DEPTH], F32)
    wpbd = kb.sb("wpbd", [128, DEPTH, 2, 128], BF16)
    corr = kb.sb("corr", [128, 32], F32)
    sT = kb.sb("sT", [128, 40], F32)
    gfin = kb.sb("gfin", [128, KC], F32)
    epsT = kb.sb("epsT", [128, 1], F32)
    const_b = Buf("consts")
    psall = kb.es.enter_context(nc.psum_tensor("psall", [128, 8, 512], F32))
    kb.psall = psall
    kb.ps = [psall[:, i, :] for i in range(8)]
    kb.psb = [Buf(f"ps{i}", excl=True) for i in range(8)]
    kb.ringA_i = 0
    kb.ringB_i = 0
    kb.arena_dma_evs = []
    kb.phase_snap = {}
    kb.arena_off = 0

    c_const = Counter(kb, "cconst")
    c_lamb = Counter(kb, "clamb")
    c_rows = Counter(kb, "crows")
    c_wp = Counter(kb, "cwp")
    c_cast = [Counter(kb, f"ccast{l}") for l in range(depth)]
    c_stage = [Counter(kb, f"cstage{i}") for i in range(2)]
    c_wada = [Counter(kb, f"cwada{i}") for i in range(3)]
    c_out = [Counter(kb, f"cout{i}") for i in range(2)]

    wp_b = Buf("wpbd")
    op(DVE, lambda: nc.vector.memset(wpbd[:], 0.0), writes=[wp_b])
    for l in range(depth):
        for g in range(4):
            ci, hb = g // 2, (g % 2) * 64
            dma(PQ, wpbd[hb:hb + 64, l, ci, hb:hb + 64], wpool_d[l, g], c_wp, writes=[wp_b], nochain=True)


    def cast_block(l, dst2d, src2d, nk):
        dma(PQ, dst2d.rearrange("p (k m) -> p k m", k=nk), src2d.rearrange("(k p) m -> p k m", p=128),
            c_cast[l], writes=[kb.wsc_buf[l]], nochain=True)

    def emit_casts(l):
        for key in tiles:
            ti = kb.tile_index[key]
            dst = kb.wsc[l][ti]
            kind = key[0]
            if CAST_ONLY is not None and kind not in CAST_ONLY:
                continue
            if kind == "cgx":
                i = key[1]
                cast_block(l, dst[:, 0:1024], w_in_d[l][:, 1792 + i * 128:1792 + (i + 1) * 128], 8)
                cast_block(l, dst[:, 1024:2048], w_in_d[l][:, 2048 + i * 128:2048 + (i + 1) * 128], 8)
            elif kind == "bg":
                for i in range(2):
                    cast_block(l, dst[:, i * 1024:(i + 1) * 1024], w_in_d[l][:, 1536 + i * 128:1536 + (i + 1) * 128], 8)
            elif kind == "pool":
                for i in range(2):
                    cast_block(l, dst[:, i * 1024:(i + 1) * 1024], w_in_d[l][:, 2304 + i * 128:2304 + (i + 1) * 128], 8)
            elif kind == "wo":
                half, cog = key[1], key[2]
                rb = 512 if half == 0 else 0
                for co in range(4):
                    cc = (cog * 4 + co) * 128
                    cast_block(l, dst[:, co * 512:(co + 1) * 512], w_out_d[l][rb:rb + 512, cc:cc + 128], 4)
            elif kind == "qk":
                h = key[1]
                cast_block(l, dst[:, 0:1024], w_in_d[l][:, h * 128:(h + 1) * 128], 8)
                cast_block(l, dst[:, 1024:2048], w_in_d[l][:, 512 + h * 128:512 + (h + 1) * 128], 8)
            elif kind == "v":
                h = key[1]
                cast_block(l, dst[:, 0:1024], w_in_d[l][:, 1024 + h * 128:1024 + (h + 1) * 128], 8)
            elif kind == "gu":
                j = key[1]
                cast_block(l, dst[:, 0:1024], w_gu_d[l][:, j * 128:(j + 1) * 128], 8)
                cast_block(l, dst[:, 1024:2048], w_gu_d[l][:, FFN + j * 128:FFN + (j + 1) * 128], 8)
            elif kind == "dn":
                half, jt = key[1], key[2]
                nj = 4 if jt < 5 else 2
                dma(PQ, dst[:, 0:nj * 512].rearrange("p (j n) -> p j n", j=nj),
                    w_dn_d[l][jt * 512:jt * 512 + nj * 128, half * 512:(half + 1) * 512].rearrange(
                        "(j p) n -> p j n", p=128),
                    c_cast[l], writes=[kb.wsc_buf[l]], nochain=True)

    emit_casts(0)
    if STOP_AT == "cast":
        for l in range(depth):
            PQ.wait_for(kb.wsc_buf[l].w)
    ckpt(kb, "cast")
    kb.new_phase()
    dma(SP, ident[:], ident_d, c_const, writes=[const_b])
    dma(SP, cosT[:], cos_d, c_const, writes=[const_b])
    dma(SP, sinT[:], sin_d, c_const, writes=[const_b])
    dma(SP, perm[:], perm_d, c_const, writes=[const_b])
    dma(SP, corr[:], corr_d, c_const, writes=[const_b])
    op(DVE, lambda: nc.vector.memset(epsT[:], EPS), writes=[const_b])
    op(DVE, lambda: nc.vector.memset(ones_bf[:], 1.0), writes=[const_b])
    op(DVE, lambda: nc.vector.memset(ones_f32[:], 1.0), writes=[const_b])
    op(DVE, lambda: nc.vector.memset(onesD[:], 1.0 / D), writes=[const_b])
    op(DVE, lambda: nc.vector.memset(ones128[:], 1.0 / 128), writes=[const_b])
    rows = kb.aalloc(128, F32)
    rows_b = kb.abuf("rows")
    lamb = kb.aalloc(4 * DEPTH * 64, F32)
    lamb_b = kb.abuf("lamb")
    lamb4 = lamb.rearrange("p (a l e) -> p a l e", a=4, l=DEPTH)
    for a in range(4):
        dma(SP, lamb4[:, a, :, :].rearrange("p l e -> p (l e)"),
            lam_d[a].rearrange("l e -> (l e)").partition_broadcast(128),
            c_lamb, writes=[lamb_b])

    def load_rows_and_transpose(srcs, dst_ap, ncols):
        op(DVE, lambda: nc.vector.memset(rows, 0.0), writes=[rows_b])
        for (r0, nr, src) in srcs:
            dma(SP, rows[r0:r0 + nr, :], src, c_rows, writes=[rows_b])
        pt, ptb = kb.psA()
        op(PE, lambda: nc.tensor.transpose(pt[:, 0:128], rows, ident[:]), reads=[rows_b, const_b], writes=[ptb])
        op(DVE, lambda: nc.vector.tensor_copy(dst_ap, pt[:, 0:ncols]), reads=[ptb], writes=[const_b])

    def r128(ap1d, n):
        return ap1d.rearrange("(k p) -> k p", p=128)

    for l in range(depth):
        srcs = [(0, 8, r128(g1_d[l], 8)), (8, 8, r128(g2_d[l], 8)), (16, 48, r128(b_ada_d[l], 48)),
                (64, 2, r128(spool_d[l], 2)), (66, 1, r128(gsub_d[l], 1))]
        for k in range(3):
            srcs.append((67 + 2 * k, 2, r128(wconv_d[l, k], 2)))
        load_rows_and_transpose(srcs, VT[:, l, :], NVEC)
    srcs = [(b * 8, 8, r128(c_d[b], 8)) for b in range(nseq)]
    srcs.append((32, 8, r128(cctx_d, 8)))
    load_rows_and_transpose(srcs, sT[:], 40)
    load_rows_and_transpose([(0, 8, r128(gfin_d, 8))], gfin[:], 8)
    op(ACT, lambda: nc.scalar.activation(sT[:], sT[:], AF.Silu), reads=[const_b], writes=[const_b])
    sT3 = sT[:].rearrange("p (b k) -> p k b", k=8)

    prod = kb.aalloc(2 * DEPTH * 64, F32)
    prod_b = kb.abuf("prod")
    prod3 = prod.rearrange("p (a l e) -> p a l e", a=2, l=DEPTH)
    esum = kb.aalloc(2 * DEPTH, F32)
    esum_b = kb.abuf("esum")
    esum2 = esum.rearrange("p (a l) -> p a l", a=2)
    for a in range(2):
        op(DVE, lambda a=a: nc.vector.tensor_tensor(prod3[:, a], lamb4[:, 2 * a], lamb4[:, 2 * a + 1], ALU.mult),
           reads=[lamb_b], writes=[prod_b])
    for a in range(2):
        op(DVE, lambda a=a: nc.vector.reduce_sum(esum2[:, a, :], prod3[:, a], AX.X), reads=[prod_b], writes=[esum_b])
    op(ACT, lambda: nc.scalar.activation(esum, esum, AF.Exp), reads=[esum_b], writes=[esum_b])
    op(DVE, lambda: nc.vector.tensor_tensor(esum2[:, 0, :], esum2[:, 0, :], esum2[:, 1, :], ALU.subtract),
       reads=[esum_b], writes=[esum_b])
    for l in range(depth):
        lam_init = 0.8 - 0.6 * math.exp(-0.3 * l)
        op(DVE, lambda l=l, li=lam_init: nc.vector.tensor_scalar(
            neglam[:, l:l + 1], esum2[:, 0, l:l + 1], -1.0, -li, ALU.mult, ALU.add),
           reads=[esum_b], writes=[const_b])
        op(DVE, lambda l=l, li=lam_init: nc.vector.tensor_scalar(
            gs[:, l:l + 1], VT[:, l, 66:67], 1.0 - li, None, ALU.mult), reads=[const_b], writes=[const_b])

    sTb = kb.aalloc(40, BF16)
    sTb_b = kb.abuf("sTb")
    op(DVE, lambda: nc.vector.tensor_copy(sTb, sT[:]), reads=[const_b], writes=[sTb_b])
    sTb3 = sTb.rearrange("p (b k) -> p k b", k=8)
    NWA = 3
    wada = [kb.aalloc(KC * 512, BF16).rearrange("p (k n) -> p k n", k=KC) for _ in range(NWA)]
    wada_b = [kb.abuf(f"wada{i}") for i in range(NWA)]
    it = 0
    for l in range(depth):
        pm, pmb = kb.psB()
        for jg in range(12):
            sl = it % NWA
            it += 1
            dma(PQ, wada[sl], w_ada_d[l][:, jg * 512:(jg + 1) * 512].rearrange("(k p) n -> p k n", p=128),
                c_wada[sl], writes=[wada_b[sl]])

            def mm(l=l, jg=jg, sl=sl):
                ins = None
                for jj in range(4):
                    j = jg * 4 + jj
                    for k in range(KC):
                        ins = nc.tensor.matmul(pm[:, j * 5:(j + 1) * 5], wada[sl][:, k, jj * 128:(jj + 1) * 128],
                                               sTb3[:, k, :], start=(k == 0), stop=(k == KC - 1))
                return ins
            op(PE, mm, reads=[wada_b[sl], sTb_b], writes=[pmb])
        op(DVE, lambda l=l: nc.vector.tensor_tensor(
            modT[:, l], pm[:, 0:240].rearrange("p (j b) -> p j b", b=5),
            VT[:, l, 16:64].unsqueeze(2).to_broadcast([128, 48, 5]), ALU.add),
           reads=[pmb, const_b], writes=[const_b])
        for w_, (sc0, g0) in enumerate(((8, 0), (32, 8))):
            for k in range(KC):
                op(DVE, lambda l=l, w_=w_, k=k, sc0=sc0, g0=g0: nc.vector.tensor_scalar(
                    AB[:, l, w_, k, :], modT[:, l, sc0 + k, :], 1.0, VT[:, l, g0 + k:g0 + k + 1],
                    ALU.add, ALU.mult), reads=[const_b], writes=[const_b])
    for l in range(1, depth):
        emit_casts(l)

    ckpt(kb, "pro")
    kb.plan = make_plan(nseq, depth)
    kb.wpos = 0
    kb.wissued = 0

    def modcol(l, j, b):
        return modT[:, l, j, b:b + 1]

    def rsqrt_ps(dst, dst_b, src, src_b):
        op(ACT, lambda: nc.scalar.activation(dst, src, AF.Ln, bias=epsT[:, 0:1]), reads=[src_b, const_b], writes=[dst_b])
        op(ACT, lambda: nc.scalar.activation(dst, dst, AF.Exp, scale=-0.5), reads=[dst_b], writes=[dst_b])

    def norm_alloc():
        sq = [kb.aalloc(512, BF16) for _ in range(8)]
        sq_b = [kb.abuf(f"sq{i}") for i in range(8)]
        rstd = [kb.aalloc(512, F32) for _ in range(2)]
        rstd_b = [kb.abuf(f"rstd{i}") for i in range(2)]
        tmp = [kb.aalloc(512, F32) for _ in range(3)]
        tmp_b = [kb.abuf(f"tmp{i}") for i in range(3)]
        return sq, sq_b, rstd, rstd_b, tmp, tmp_b, [0, 0]

    def norm_tile_stages(l, s, which, t, tmps, final=False, yT=None, yT_b=None):
        sq, sq_b, rstd, rstd_b, tmp, tmp_b, cnt = tmps
        t0, t1 = TT[t]
        w = t1 - t0
        b = s if t < 4 else 4
        r = cnt[0] % 2
        cnt[0] += 1

        def s1():
            for k in range(KC):
                if k < 6:
                    op(DVE, lambda k=k: nc.vector.tensor_tensor(sq[k][:, 0:w], xT[:, k, t0:t1], xT[:, k, t0:t1], ALU.mult),
                       reads=[xT_b[k][t]], writes=[sq_b[k]])
                else:
                    op(ACT, lambda k=k: nc.scalar.activation(sq[k][:, 0:w], xT[:, k, t0:t1], AF.Square),
                       reads=[xT_b[k][t]], writes=[sq_b[k]])

        def s2():
            pa, pab = kb.psA()

            def mm():
                ins = None
                for k in range(KC):
                    ins = nc.tensor.matmul(pa[:, 0:w], onesD[:], sq[k][:, 0:w], start=(k == 0), stop=(k == KC - 1))
                return ins
            op(PE, mm, reads=sq_b + [const_b], writes=[pab])
            rsqrt_ps(rstd[r][:, 0:w], rstd_b[r], pa[:, 0:w], pab)

        def s3():
            for k in range(KC):
                i = cnt[1] % 3
                cnt[1] += 1
                op(DVE, lambda k=k, i=i: nc.vector.tensor_tensor(tmp[i][:, 0:w], xT[:, k, t0:t1], rstd[r][:, 0:w], ALU.mult),
                   reads=[xT_b[k][t], rstd_b[r]], writes=[tmp_b[i]])
                if final:
                    op(ACT, lambda k=k, i=i: nc.scalar.activation(
                        yT[:, k, 0:w], tmp[i][:, 0:w], AF.Identity, scale=gfin[:, k:k + 1]),
                       reads=[tmp_b[i], const_b], writes=[yT_b[k][0]])
                else:
                    sh = 0 if which == 0 else 24
                    op(ACT, lambda k=k, i=i: nc.scalar.activation(
                        hT[:, k, t0:t1], tmp[i][:, 0:w], AF.Identity, bias=modcol(l, sh + k, b),
                        scale=AB[:, l, which, k, b:b + 1]),
                       reads=[tmp_b[i], const_b], writes=[hT_b[k][t]])
        return [s1, s2, s3]

    def norm_phase(l, s, which, tts, final=False, yT=None, yT_b=None, tmps=None):
        tm = tmps if tmps is not None else norm_alloc()
        for t in tts:
            for st_ in norm_tile_stages(l, s, which, t, tm, final=final, yT=yT, yT_b=yT_b):
                st_()

    def proj_group(wt_ap, t, wtb, extra_reads=()):
        t0, t1 = TT[t]
        w = t1 - t0
        pa, pab = kb.psA()

        def mm():
            ins = None
            for k in range(KC):
                ins = nc.tensor.matmul(pa[:, 0:w], wt_ap[:, k * 128:(k + 1) * 128], hT[:, k, t0:t1],
                                       start=(k == 0), stop=(k == KC - 1))
            return ins
        op(PE, mm, reads=[wtb] + [hT_b[k][t] for k in range(KC)] + list(extra_reads), writes=[pab])
        return pa, pab, w

    def outproj(l, s, half, mix, mix_b, tts):
        for cog in range(2):
            wt, wtb = kb.wnext(l, ("wo", half, cog))
            for co in range(4):
                c = cog * 4 + co
                for t in tts:
                    t0, t1 = TT[t]
                    w = t1 - t0
                    b = s if t < 4 else 4
                    pa, pab = kb.psA()

                    def mm(co=co, t0=t0, t1=t1, w=w, pa=pa):
                        ins = None
                        for kc in range(4):
                            ins = nc.tensor.matmul(pa[:, 0:w], wt[:, (co * 4 + kc) * 128:(co * 4 + kc + 1) * 128],
                                                   mix[:, kc, t0:t1], start=(kc == 0), stop=(kc == 3))
                        return ins
                    op(PE, mm, reads=[wtb] + [mix_b[kc][t] for kc in range(4)], writes=[pab])
                    op(DVE, lambda c=c, t0=t0, t1=t1, w=w, pa=pa, b=b: nc.vector.scalar_tensor_tensor(
                        xT[:, c, t0:t1], pa[:, 0:w], modcol(l, 16 + c, b), xT[:, c, t0:t1], ALU.mult, ALU.add),
                       reads=[pab, const_b, xT_b[c][t]], writes=[xT_b[c][t]])
            kb.wdone()

    def outproj_tiles(l, s, half, mix, mix_b, tts, hook):
        wts = [kb.wnext(l, ("wo", half, cog)) for cog in range(2)]
        for i_, t in enumerate(tts):
            t0, t1 = TT[t]
            w = t1 - t0
            b = s if t < 4 else 4
            for cog in range(2):
                wt, wtb = wts[cog]
                for co in range(4):
                    c = cog * 4 + co
                    pa, pab = kb.psA()

                    def mm(co=co, pa=pa, wt=wt):
                        ins = None
                        for kc in range(4):
                            ins = nc.tensor.matmul(pa[:, 0:w], wt[:, (co * 4 + kc) * 128:(co * 4 + kc + 1) * 128],
                                                   mix[:, kc, t0:t1], start=(kc == 0), stop=(kc == 3))
                        return ins
                    op(PE, mm, reads=[wtb] + [mix_b[kc][t] for kc in range(4)], writes=[pab])
                    op(DVE, lambda c=c, pa=pa: nc.vector.scalar_tensor_tensor(
                        xT[:, c, t0:t1], pa[:, 0:w], modcol(l, 16 + c, b), xT[:, c, t0:t1], ALU.mult, ALU.add),
                       reads=[pab, const_b, xT_b[c][t]], writes=[xT_b[c][t]])
            hook(i_)
        kb.wdone()
        kb.wdone()

    for s in range(nseq):
        kb.new_phase()
        stage = [kb.aalloc(D, F32) for _ in range(2)]
        stage_b = [kb.abuf(f"stage{i}") for i in range(2)]
        ntm0 = norm_alloc()
        pend0 = {}
        for i in range(18):
            sl = i % 2
            src = x_d[s, i * 128:(i + 1) * 128, :] if i < 16 else ctx_d[s, (i - 16) * 128:(i - 15) * 128, :]
            dma(SP if s == 0 else PQ, stage[sl], src, c_stage[sl], writes=[stage_b[sl]])
            t = (i * 128) // 512
            for g in range(2):
                pa, pab = kb.psA()

                def tr(g=g, sl=sl, pa=pa):
                    ins = None
                    for kk in range(4):
                        k = g * 4 + kk
                        ins = nc.tensor.transpose(pa[:, kk * 128:(kk + 1) * 128], stage[sl][:, k * 128:(k + 1) * 128],
                                                  ident[:])
                    return ins
                op(PE, tr, reads=[stage_b[sl], const_b], writes=[pab])
                E = ACT if g == 0 else DVE
                dst = xT[:, g * 4:(g + 1) * 4, i * 128:(i + 1) * 128]
                srcp = pa[:, 0:512].rearrange("p (k n) -> p k n", k=4)
                if g == 0:
                    op(ACT, lambda dst=dst, srcp=srcp: nc.scalar.copy(dst, srcp), reads=[pab],
                       writes=[xT_b[k][t] for k in range(0, 4)])
                else:
                    op(DVE, lambda dst=dst, srcp=srcp: nc.vector.tensor_copy(dst, srcp), reads=[pab],
                       writes=[xT_b[k][t] for k in range(4, 8)])
            if i in (3, 7, 11, 15, 17):
                for d_, st_ in enumerate(norm_tile_stages(0, s, 0, t, ntm0)):
                    pend0.setdefault(i + d_, []).append(st_)
            for st_ in pend0.pop(i, []):
                st_()
        for i in sorted(pend0):
            for st_ in pend0[i]:
                st_()
        if s == 0:
            for _ in range(NSLOT):
                kb.wissue()

        ckpt(kb, "load")
        n1_skip = {0, 1, 2, 3, 4}
        for l in range(depth):
            last = l == depth - 1
            tts_all = [0, 1, 2, 3, 4]
            tts_x = [0, 1, 2, 3] if last else [0, 1, 2, 3, 4]
            segs = [(0, L)] if last else [(0, L), (L, NT)]

            kb.new_phase()
            norm_phase(l, s, 0, [t for t in tts_all if t not in n1_skip])
            n1_skip = set()

            ckpt(kb, "n1")
            kb.new_phase()
            cgt = [kb.aalloc(512, F32) for _ in range(2)]
            cgt_b = [kb.abuf(f"cgt{i}") for i in range(2)]
            big = [kb.aalloc(PADW, F32) for _ in range(3)]
            big_b = [kb.abuf(f"big{i}") for i in range(3)]
            pooled = kb.aalloc(NT, BF16)
            pooled_b = [kb.abuf(f"pooled{t}") for t in range(5)]
            mixcp = kb.aalloc(4 * NT, BF16).rearrange("p (c n) -> p c n", c=4)
            mixcp_b = [[kb.abuf(f"mixcp{c}_{t}") for t in range(5)] for c in range(4)]
            for i in range(3):
                op(DVE, lambda i=i: nc.vector.memset(big[i][:, 0:8], 0.0), writes=[big_b[i]])
                op(DVE, lambda i=i: nc.vector.memset(big[i][:, 2056:2072], 0.0), writes=[big_b[i]])
                op(DVE, lambda i=i: nc.vector.memset(big[i][:, 2328:2336], 0.0), writes=[big_b[i]])

            def pseg(ap, a, b_, sh=0):
                return ap[:, padcol(a) + sh:padcol(a) + sh + (b_ - a)]

            ybuf = [big[1], big[2]]
            ybuf_b = [big_b[1], big_b[2]]
            for i in range(2):
                wt, wtb = kb.wnext(l, ("cgx", i))
                for t in tts_x:
                    t0, t1 = TT[t]
                    pc, pcb, w = proj_group(wt[:, 0:1024], t, wtb)
                    px, pxb, w = proj_group(wt[:, 1024:2048], t, wtb)
                    ci = t % 2
                    op(ACT, lambda pc=pc, ci=ci, w=w: nc.scalar.copy(cgt[ci][:, 0:w], pc[:, 0:w]),
                       reads=[pcb], writes=[cgt_b[ci]])
                    op(DVE, lambda px=px, ci=ci, w=w, t0=t0, t1=t1: nc.vector.tensor_tensor(
                        pseg(big[0], t0, t1), px[:, 0:w], cgt[ci][:, 0:w], ALU.mult),
                       reads=[pxb, cgt_b[ci]], writes=[big_b[0]])
                kb.wdone()
                wc = lambda k, i=i: VT[:, l, 67 + 2 * k + i:68 + 2 * k + i]
                for (a, b_) in segs:
                    op(DVE, lambda a=a, b_=b_, i=i: nc.vector.tensor_scalar(
                        pseg(ybuf[i], a, b_), pseg(big[0], a, b_), wc(1), None, ALU.mult),
                       reads=[big_b[0], const_b], writes=[ybuf_b[i]])
                    op(DVE, lambda a=a, b_=b_, i=i: nc.vector.scalar_tensor_tensor(
                        pseg(ybuf[i], a, b_), pseg(big[0], a, b_, -1), wc(0), pseg(ybuf[i], a, b_), ALU.mult, ALU.add),
                       reads=[big_b[0], const_b, ybuf_b[i]], writes=[ybuf_b[i]])
                    op(DVE, lambda a=a, b_=b_, i=i: nc.vector.scalar_tensor_tensor(
                        pseg(ybuf[i], a, b_), pseg(big[0], a, b_, 1), wc(2), pseg(ybuf[i], a, b_), ALU.mult, ALU.add),
                       reads=[big_b[0], const_b, ybuf_b[i]], writes=[ybuf_b[i]])
            wt, wtb = kb.wnext(l, ("bg",))
            for i in range(2):
                for t in tts_x:
                    t0, t1 = TT[t]
                    pb_, pbb, w = proj_group(wt[:, i * 1024:(i + 1) * 1024], t, wtb)
                    op(DVE, lambda pb_=pb_, i=i, t0=t0, t1=t1, w=w: nc.vector.tensor_tensor(
                        mixcp[:, i, t0:t1], pb_[:, 0:w], pseg(ybuf[i], t0, t1), ALU.mult),
                       reads=[pbb, ybuf_b[i]], writes=[mixcp_b[i][t]])
            kb.wdone()
            wt, wtb = kb.wnext(l, ("pool",))
            for ci in range(2):
                P_, A_, B_ = big[0], big[1], big[2]
                Pb, Ab, Bb = big_b[0], big_b[1], big_b[2]
                for t in tts_x:
                    t0, t1 = TT[t]
                    pp, ppb, w = proj_group(wt[:, ci * 1024:(ci + 1) * 1024], t, wtb)
                    op(ACT, lambda pp=pp, t0=t0, t1=t1, w=w: nc.scalar.copy(pseg(P_, t0, t1), pp[:, 0:w]),
                       reads=[ppb], writes=[Pb])
                if ci == 1:
                    kb.wdone()

                def shift_add(O, Ob, I, Ib, sa, sb_, ext):
                    for (a, b_) in segs:
                        c0 = padcol(a) - ext
                        n = (b_ - a) + 2 * ext
                        op(DVE, lambda c0=c0, n=n: nc.vector.tensor_tensor(
                            O[:, c0:c0 + n], I[:, c0 - sa:c0 - sa + n], I[:, c0 + sb_:c0 + sb_ + n], ALU.add),
                           reads=[Ib], writes=[Ob])
                shift_add(A_, Ab, P_, Pb, 1, 0, 7)
                shift_add(B_, Bb, A_, Ab, 1, 1, 6)
                if ci == 0:
                    sel = [(0, 64, A_, Ab, 2.0), (64, 128, B_, Bb, 4.0)]
                else:
                    shift_add(A_, Ab, B_, Bb, 2, 2, 4)
                    shift_add(B_, Bb, A_, Ab, 4, 4, 0)
                    sel = [(0, 64, A_, Ab, 8.0), (64, 128, B_, Bb, 16.0)]
                for (p0, p1, S_, Sb, wwin) in sel:
                    for (a, b_) in segs:
                        c0 = padcol(a)
                        c1 = padcol(a) + (b_ - a)
                        cc = ci * 16
                        op(DVE, lambda S_=S_, p0=p0, p1=p1, c0=c0, cc=cc: nc.vector.tensor_tensor(
                            S_[p0:p1, c0:c0 + 8], S_[p0:p1, c0:c0 + 8], corr[p0:p1, cc:cc + 8], ALU.mult),
                           reads=[const_b, Sb], writes=[Sb])
                        op(DVE, lambda S_=S_, p0=p0, p1=p1, c1=c1, cc=cc: nc.vector.tensor_tensor(
                            S_[p0:p1, c1 - 8:c1], S_[p0:p1, c1 - 8:c1], corr[p0:p1, cc + 8:cc + 16], ALU.mult),
                           reads=[const_b, Sb], writes=[Sb])
                        tl = [t for t in range(5) if TT[t][0] >= a and TT[t][1] <= b_]
                        op(DVE, lambda S_=S_, p0=p0, p1=p1, c0=c0, c1=c1, a=a, b_=b_, wwin=wwin: nc.vector.scalar_tensor_tensor(
                            pooled[p0:p1, a:b_], S_[p0:p1, c0:c1], 1.0 / wwin, P_[p0:p1, c0:c1], ALU.mult, ALU.subtract),
                           reads=[Sb, Pb], writes=[pooled_b[t] for t in tl])
                for t in tts_x:
                    t0, t1 = TT[t]
                    w = t1 - t0
                    pa, pab = kb.psA()
                    op(PE, lambda pa=pa, t0=t0, t1=t1, w=w, ci=ci: nc.tensor.matmul(
                        pa[:, 0:w], wpbd[:, l, ci, :], pooled[:, t0:t1], start=True, stop=True),
                       reads=[wp_b, pooled_b[t]], writes=[pab])
                    op(ACT, lambda pa=pa, t0=t0, t1=t1, w=w, ci=ci: nc.scalar.activation(
                        mixcp[:, 2 + ci, t0:t1], pa[:, 0:w], AF.Identity, scale=VT[:, l, 64 + ci:65 + ci]),
                       reads=[pab, const_b], writes=[mixcp_b[2 + ci][t]])
            outproj(l, s, 0, mixcp, mixcp_b, tts_x)

            ckpt(kb, "cp")
            kb.new_phase()
            qT = [kb.aalloc(NT, BF16) for _ in range(2)]
            kT = [kb.aalloc(NT, BF16) for _ in range(2)]
            qT_b = [[kb.abuf(f"qT{i}_{t}") for t in range(5)] for i in range(2)]
            kT_b = [[kb.abuf(f"kT{i}_{t}") for t in range(5)] for i in range(2)]
            Vh = kb.aalloc(18 * 128, BF16).rearrange("p (i n) -> p i n", i=18)
            Vh_b = [kb.abuf(f"Vh{t}") for t in range(5)]
            Pt = [kb.aalloc(1024, BF16).rearrange("p (a n) -> p a n", a=2) for _ in range(3)]
            Pt_b = [kb.abuf(f"P{i}") for i in range(3)]
            deferred = []
            qraw = [kb.aalloc(512, BF16) for _ in range(1)]
            qraw_b = [kb.abuf(f"qraw{i}") for i in range(1)]
            t1b = kb.aalloc(512, F32)
            t1b_b = kb.abuf("t1")
            t2b = kb.aalloc(512, F32)
            t2b_b = kb.abuf("t2")
            rr = kb.aalloc(1024, F32).rearrange("p (a n) -> p a n", a=2)
            rr_b = kb.abuf("rr")
            racc = [kb.aalloc(1024, F32).rearrange("p (a n) -> p a n", a=2) for _ in range(2)]
            racc_b = [kb.abuf(f"racc{i}") for i in range(2)]
            qti = 0
            oo = [kb.aalloc(512, F32) for _ in range(2)]
            oo_b = [kb.abuf(f"oo{i}") for i in range(2)]
            sqa = kb.aalloc(512, BF16)
            sqa_b = kb.abuf("sqa")
            rsa = rr[:, 0, :]
            rsa_b = rr_b
            mixat = kb.aalloc(4 * NT, BF16).rearrange("p (c n) -> p c n", c=4)
            mixat_b = [[kb.abuf(f"mixat{c}_{t}") for t in range(5)] for c in range(4)]
            pti = 0
            qri = [0]
            bg = []

            def qk_items(h, bi):
                wt, wtb = kb.wnext(l, ("qk", h))
                items = []
                todo = [(which, t) for which in range(2) for t in tts_all if not (which == 0 and t == 4 and last)]
                for idx, (which, t) in enumerate(todo):
                    def stage1(which=which, t=t, islast=(idx == len(todo) - 1), bgbanks=False):
                        dstT, dst_b = (qT[bi], qT_b[bi]) if which == 0 else (kT[bi], kT_b[bi])
                        t0, t1 = TT[t]
                        w = t1 - t0
                        if bgbanks:
                            i0_ = kb.ringB_i
                            pq, pqb = kb.ps[4 + i0_], kb.psb[4 + i0_]
                            p2, p2b = kb.ps[4 + (i0_ + 1) % 4], kb.psb[4 + (i0_ + 1) % 4]
                        else:
                            pq, pqb = kb.psA()
                            p2, p2b = kb.psA()
                        wsl = wt[:, which * 1024:(which + 1) * 1024]

                        def mm():
                            ins = None
                            for k in range(KC):
                                ins = nc.tensor.matmul(pq[:, 0:w], wsl[:, k * 128:(k + 1) * 128], hT[:, k, t0:t1],
                                                       start=(k == 0), stop=(k == KC - 1))
                            return ins
                        op(PE, mm, reads=[wtb] + [hT_b[k][t] for k in range(KC)], writes=[pqb])
                        if islast:
                            kb.wdone()
                        if t < 4:
                            op(DVE, lambda: nc.vector.tensor_copy(qraw[0][:], pq[:]), reads=[pqb], writes=[qraw_b[0]])
                            op(DVE, lambda: nc.vector.tensor_tensor(t1b, pq[:], cosT[:, t0:t1], ALU.mult),
                               reads=[pqb, const_b], writes=[t1b_b])

                            def stage2():
                                op(PE, lambda: nc.tensor.matmul(p2[:], perm[:], qraw[0][:], start=True, stop=True),
                                   reads=[qraw_b[0], const_b], writes=[p2b])
                                op(DVE, lambda: nc.vector.tensor_tensor(t2b, p2[:], sinT[:, t0:t1], ALU.mult),
                                   reads=[p2b, const_b], writes=[t2b_b])
                                op(DVE, lambda: nc.vector.tensor_tensor(dstT[:, t0:t1], t1b, t2b, ALU.add),
                                   reads=[t1b_b, t2b_b], writes=[dst_b[t]])
                            return stage2
                        op(DVE, lambda: nc.vector.tensor_copy(dstT[:, t0:t1], pq[:, 0:w]), reads=[pqb], writes=[dst_b[t]])
                        return None
                    items.append(stage1)
                return items

            for it_ in qk_items(0, 0):
                s2_ = it_()
                if s2_:
                    s2_()
            carry = []
            for h in range(4):
                hb = h % 2
                wtv, wtvb = kb.wnext(l, ("v", h))
                for t in tts_all:
                    t0, t1 = TT[t]
                    ni = (t1 - t0) // 128
                    i0 = t0 // 128
                    pa, pab = kb.psA()

                    def mmv(pa=pa, ni=ni, i0=i0):
                        ins = None
                        for ii in range(ni):
                            tok = (i0 + ii) * 128
                            for k in range(KC):
                                ins = nc.tensor.matmul(pa[:, ii * 128:(ii + 1) * 128], hT[:, k, tok:tok + 128],
                                                       wtv[:, k * 128:(k + 1) * 128], start=(k == 0), stop=(k == KC - 1))
                        return ins
                    op(PE, mmv, reads=[wtvb] + [hT_b[k][t] for k in range(KC)], writes=[pab])
                    op(DVE, lambda pa=pa, ni=ni, i0=i0: nc.vector.tensor_copy(
                        Vh[:, i0:i0 + ni, :], pa[:, 0:ni * 128].rearrange("p (i n) -> p i n", i=ni)),
                       reads=[pab], writes=[Vh_b[t]])
                kb.wdone()

                ckpt(kb, "at_v")
                if h + 1 < 4:
                    bg.extend(qk_items(h + 1, (h + 1) % 2))
                qtiles = [0, 1, 2, 3] if last else [0, 1, 2, 3, 4]
                for qt in qtiles:
                    q0, q1 = TT[qt]
                    w = q1 - q0
                    kts = list(range(18)) if qt < 4 else [16, 17]
                    nk = len(kts)
                    sched = {}
                    for off, c_ in carry:
                        sched.setdefault(off, []).append(c_)
                    carry = []
                    if nk == 18:
                        for o1_, o2_ in ((4, 7), (9, 11), (13, 15)):
                            if bg:
                                it_ = bg.pop(0)

                                def run1(it_=it_, o2_=o2_):
                                    s2_ = it_(bgbanks=True)
                                    if s2_:
                                        sched.setdefault(o2_, []).append(s2_)
                                sched.setdefault(o1_, []).append(run1)
                    O0, O0b = kb.psB()
                    O1, O1b = kb.psB()
                    ai = qti % 2
                    qti += 1
                    acc, accb = racc[ai], racc_b[ai]
                    ntail = 2 if nk == 18 else nk
                    tailP = []

                    def score(kt, q0=q0, q1=q1, w=w):
                        sp2, spbs = kb.psA2()

                        def mm(sp2=sp2, kt=kt):
                            ins = None
                            for m in range(2):
                                ins = nc.tensor.matmul(
                                    sp2[:, m, 0:w], kT[hb][m * 64:(m + 1) * 64, kt * 128:(kt + 1) * 128],
                                    qT[hb][m * 64:(m + 1) * 64, q0:q1], start=True, stop=True)
                            return ins
                        op(PE, mm, reads=[kT_b[hb][kt // 4], qT_b[hb][qt]], writes=spbs)
                        return sp2, spbs
                    cur = score(kts[0])
                    pr2 = prbs = None
                    for n_, kt in enumerate(kts):
                        nxt = score(kts[n_ + 1]) if n_ + 1 < nk else None
                        sp2, spbs = cur
                        pi = pti % 3
                        pti += 1
                        op(ACT, lambda sp2=sp2, pi=pi, w=w: nc.scalar.activation(
                            Pt[pi][:, :, 0:w], sp2[:, :, 0:w], AF.Exp, scale=0.125), reads=spbs, writes=[Pt_b[pi]])

                        def pv(kt=kt, pi=pi, n_=n_, w=w):
                            nc.tensor.matmul(O0[:, 0:w], Vh[:, kt, :], Pt[pi][:, 0, 0:w], start=(n_ == 0), stop=(n_ == nk - 1))
                            return nc.tensor.matmul(O1[:, 0:w], Vh[:, kt, :], Pt[pi][:, 1, 0:w], start=(n_ == 0),
                                                    stop=(n_ == nk - 1))
                        op(PE, pv, reads=[Vh_b[kt // 4], Pt_b[pi]], writes=[O0b, O1b])
                        if n_ < nk - ntail:
                            if n_ == 0:
                                op(DVE, lambda pi=pi, w=w: nc.vector.tensor_copy(acc[:, :, 0:w], Pt[pi][:, :, 0:w]),
                                   reads=[Pt_b[pi]], writes=[accb])
                            else:
                                op(DVE, lambda pi=pi, w=w: nc.vector.tensor_tensor(acc[:, :, 0:w], acc[:, :, 0:w],
                                                                                  Pt[pi][:, :, 0:w], ALU.add),
                                   reads=[Pt_b[pi], accb], writes=[accb])
                        else:
                            if pr2 is None:
                                pr2, prbs = kb.psA2()
                            first = (n_ == nk - ntail)
                            lastt = (n_ == nk - 1)

                            def mmr(pr2=pr2, w=w, pi=pi, first=first, lastt=lastt, hasacc=(nk - ntail > 0)):
                                ins = None
                                for m in range(2):
                                    if first and hasacc:
                                        nc.tensor.matmul(pr2[:, m, 0:w], ones_f32[:], acc[:, m, 0:w], start=True, stop=False)
                                    ins = nc.tensor.matmul(pr2[:, m, 0:w], ones_bf[:], Pt[pi][:, m, 0:w],
                                                           start=(first and not hasacc), stop=lastt)
                                return ins
                            op(PE, mmr, reads=[accb, Pt_b[pi], const_b], writes=prbs)
                        cur = nxt
                        for c_ in sched.pop(n_, []):
                            c_()
                    for off in sorted(sched):
                        for c_ in sched[off]:
                            c_()
                    ckpt(kb, "at_s")
                    op(ACT, lambda pr2=pr2, w=w: nc.scalar.activation(rr[:, :, 0:w], pr2[:, :, 0:w], AF.Ln),
                       reads=prbs, writes=[rr_b])
                    op(ACT, lambda w=w: nc.scalar.activation(rr[:, :, 0:w], rr[:, :, 0:w], AF.Exp, scale=-1.0),
                       reads=[rr_b], writes=[rr_b])

                    def stB(w=w, O0=O0, O1=O1, O0b=O0b, O1b=O1b):
                        op(DVE, lambda: nc.vector.tensor_tensor(oo[0][:, 0:w], O0[:, 0:w], rr[:, 0, 0:w], ALU.mult),
                           reads=[O0b, rr_b], writes=[oo_b[0]])
                        op(DVE, lambda: nc.vector.scalar_tensor_tensor(
                            oo[1][:, 0:w], O1[:, 0:w], neglam[:, l:l + 1], rr[:, 1, 0:w], ALU.mult, ALU.mult),
                           reads=[O1b, rr_b, const_b], writes=[oo_b[1]])
                        op(DVE, lambda: nc.vector.tensor_tensor(oo[0][:, 0:w], oo[0][:, 0:w], oo[1][:, 0:w], ALU.add),
                           reads=[oo_b[0], oo_b[1]], writes=[oo_b[0]])
                        op(DVE, lambda: nc.vector.tensor_tensor(sqa[:, 0:w], oo[0][:, 0:w], oo[0][:, 0:w], ALU.mult),
                           reads=[oo_b[0]], writes=[sqa_b])

                    def stC(w=w, pm_=O1, pmb_=O1b):
                        op(PE, lambda: nc.tensor.matmul(pm_[:, 0:w], ones128[:], sqa[:, 0:w], start=True, stop=True),
                           reads=[sqa_b, const_b], writes=[pmb_])
                        rsqrt_ps(rsa[:, 0:w], rsa_b, pm_[:, 0:w], pmb_)

                    def stD(w=w, q0=q0, q1=q1, h=h, qt=qt):
                        op(DVE, lambda: nc.vector.scalar_tensor_tensor(
                            mixat[:, h, q0:q1], oo[0][:, 0:w], gs[:, l:l + 1], rsa[:, 0:w], ALU.mult, ALU.mult),
                           reads=[oo_b[0], rsa_b, const_b], writes=[mixat_b[h][qt]])
                    carry = [(2, stB), (6, stC), (10, stD)]
                while bg:
                    s2_ = bg.pop(0)()
                    if s2_:
                        s2_()
            for off, c_ in carry:
                c_()
            kb.phase_snap = {E.counter: E.counter.cur() for E in (PE, ACT, DVE, POOL)}
            saved_off_ = kb.arena_off
            kb.arena_off = 0
            ntm_pre = norm_alloc()
            assert kb.arena_off <= 51712
            kb.arena_off = saved_off_
            pre_st = {}
            for d_, st_ in enumerate(norm_tile_stages(l, s, 1, 0, ntm_pre)):
                pre_st.setdefault(0 + d_, []).append(st_)
            for d_, st_ in enumerate(norm_tile_stages(l, s, 1, 1, ntm_pre)):
                pre_st.setdefault(1 + d_, []).append(st_)

            def pre_hook(i_):
                for st_ in pre_st.pop(i_, []):
                    st_()
            outproj_tiles(l, s, 1, mixat, mixat_b, tts_x, pre_hook)
            for i_ in sorted(pre_st):
                for st_ in pre_st[i_]:
                    st_()

            ckpt(kb, "at")
            kb.new_phase()
            ntm = norm_alloc()
            actT = kb.aalloc(NJ * 1024, BF16).rearrange("p (j n) -> p j n", j=NJ)
            actT_b = [[kb.abuf(f"act{j}_{u}") for u in range(2)] for j in range(NJ)]
            sg = [kb.aalloc(512, F32) for _ in range(2)]
            sg_b = [kb.abuf(f"sg{i}") for i in range(2)]
            sgi = 0
            side = {}

            def add_side(ti_, j0, stages):
                for d_, st_ in enumerate(stages):
                    side.setdefault((ti_, j0 + d_), []).append(st_)
            for idx_, t in enumerate([t for t in tts_x if t >= 2]):
                add_side(0, 1 + 6 * idx_, norm_tile_stages(l, s, 1, t, ntm))
            n1_skip = set()
            if not last:
                add_side(1, 2, norm_tile_stages(l + 1, s, 0, 0, ntm))
                add_side(1, 10, norm_tile_stages(l + 1, s, 0, 1, ntm))
                add_side(2, 2, norm_tile_stages(l + 1, s, 0, 2, ntm))
                add_side(2, 10, norm_tile_stages(l + 1, s, 0, 3, ntm))
                n1_skip = {0, 1, 2, 3}
            ckpt(kb, "n2")
            for ti_, tok in enumerate(ffn_token_tiles(last)):
                for j in range(NJ):
                    wt, wtb = kb.wnext(l, ("gu", j))
                    for u, (t0, t1) in enumerate(tok):
                        t = TT.index((t0, t1))
                        w = t1 - t0
                        pg, pgb, _ = proj_group(wt[:, 0:1024], t, wtb)
                        pu, pub, _ = proj_group(wt[:, 1024:2048], t, wtb)
                        si = sgi % 2
                        sgi += 1
                        op(ACT, lambda pg=pg, si=si, w=w: nc.scalar.activation(sg[si][:, 0:w], pg[:, 0:w], AF.Silu),
                           reads=[pgb], writes=[sg_b[si]])
                        op(DVE, lambda pu=pu, si=si, w=w, j=j, u=u: nc.vector.tensor_tensor(
                            actT[:, j, u * 512:u * 512 + w], pu[:, 0:w], sg[si][:, 0:w], ALU.mult),
                           reads=[pub, sg_b[si]], writes=[actT_b[j][u]])
                    kb.wdone()
                    for c_ in side.pop((ti_, j), []):
                        c_()
                for u, (t0, t1) in enumerate(tok):
                    t = TT.index((t0, t1))
                    w = t1 - t0
                    b = s if t < 4 else 4
                    for half in range(2):
                        acc = [kb.psB() for _ in range(4)]
                        for jt in range(6):
                            wt, wtb = kb.wnext(l, ("dn", half, jt))
                            nj = 4 if jt < 5 else 2
                            for jj in range(nj):
                                j = jt * 4 + jj

                                def mmd(j=j, jj=jj, u=u, w=w, acc=acc, wt=wt):
                                    ins = None
                                    for c in range(4):
                                        ins = nc.tensor.matmul(acc[c][0][:, 0:w],
                                                               wt[:, jj * 512 + c * 128:jj * 512 + (c + 1) * 128],
                                                               actT[:, j, u * 512:u * 512 + w],
                                                               start=(j == 0), stop=(j == NJ - 1))
                                    return ins
                                op(PE, mmd, reads=[wtb, actT_b[j][u]], writes=[a[1] for a in acc])
                            kb.wdone()
                        for c in range(4):
                            co = half * 4 + c
                            op(DVE, lambda c=c, co=co, t0=t0, t1=t1, w=w, acc=acc, b=b: nc.vector.scalar_tensor_tensor(
                                xT[:, co, t0:t1], acc[c][0][:, 0:w], modcol(l, 40 + co, b), xT[:, co, t0:t1],
                                ALU.mult, ALU.add),
                               reads=[acc[c][1], const_b, xT_b[co][t]], writes=[xT_b[co][t]])

        ckpt(kb, "ffn")
        kb.new_phase()
        yT = kb.aalloc(KC * 512, F32).rearrange("p (k n) -> p k n", k=KC)
        ost = [kb.aalloc(D, F32) for _ in range(2)]
        ost_b = [kb.abuf(f"ost{i}") for i in range(2)]
        ntm = norm_alloc()
        yT_b = [[kb.abuf(f"yT{k}")] for k in range(KC)]
        oi = 0
        for t in range(4):
            norm_phase(depth - 1, s, 0, [t], final=True, yT=yT, yT_b=yT_b, tmps=ntm)
            for ii in range(4):
                sl = oi % 2
                oi += 1
                tok = t * 512 + ii * 128
                for g in range(2):
                    pa, pab = kb.psA()

                    def tr(g=g, ii=ii, pa=pa):
                        ins = None
                        for kk in range(4):
                            k = g * 4 + kk
                            ins = nc.tensor.transpose(pa[:, kk * 128:(kk + 1) * 128], yT[:, k, ii * 128:(ii + 1) * 128],
                                                      ident[:])
                        return ins
                    op(PE, tr, reads=[yT_b[k][0] for k in range(g * 4, g * 4 + 4)] + [const_b], writes=[pab])
                    if g == 0:
                        op(ACT, lambda pa=pa, sl=sl: nc.scalar.copy(ost[sl][:, 0:512], pa[:]), reads=[pab], writes=[ost_b[sl]])
                    else:
                        op(DVE, lambda pa=pa, sl=sl: nc.vector.tensor_copy(ost[sl][:, 512:1024], pa[:]), reads=[pab],
                           writes=[ost_b[sl]])
                ev = dma(PQ, out_d[s, tok:tok + 128, :], ost[sl], c_out[sl], reads=[ost_b[sl]])
                kb.arena_dma_evs.append(ev)
                kb.final_out_evs = getattr(kb, "final_out_evs", {})
                kb.final_out_evs[c_out[sl]] = ev

    for ev in kb.final_out_evs.values():
        PQ.wait_for(ev)
    assert kb.wpos == len(kb.plan), (kb.wpos, len(kb.plan))


def host_consts():
    i = np.arange(128)
    i64 = i % 64
    a = i64 // 32
    half = (i64 % 32) // 16
    f = i64 % 16
    t = np.arange(L)
    row = (t // 64).astype(np.float32)
    col = (t % 64).astype(np.float32)
    inv = (1.0 / (10000.0 ** (np.arange(16, dtype=np.float32) / 16))).astype(np.float32)
    pos = np.where(a[:, None] == 0, row[None, :], col[None, :]).astype(np.float32)
    ang = (pos * inv[f][:, None]).astype(np.float32)
    cos = np.cos(ang).astype(np.float32)
    sin = np.sin(ang).astype(np.float32)
    sin_s = np.where(half[:, None] == 0, -sin, sin).astype(np.float32)
    perm = np.zeros((128, 128), np.float32)
    perm[i ^ 16, i] = 1.0
    corr = np.ones((128, 32), np.float32)
    for ci in range(2):
        for p in range(128):
            w = (2, 4, 8, 16)[ci * 2 + (p // 64)]
            for c in range(8):
                lo = max(c - w // 2, 0)
                hi = c + w - 1 - w // 2
                corr[p, ci * 16 + c] = w / float(hi - lo + 1)
                d = 7 - c
                hi_off = min(w - 1 - w // 2, d)
                cnt = w // 2 + hi_off + 1
                corr[p, ci * 16 + 8 + c] = w / float(cnt)
    return {
        "k_ident": np.eye(128, dtype=np.float32),
        "k_cos": cos.astype(ml_dtypes.bfloat16),
        "k_sin": sin_s.astype(ml_dtypes.bfloat16),
        "k_perm": perm.astype(ml_dtypes.bfloat16),
        "k_corr": corr,
    }


_PROG_CACHE = {}


def run(inputs, nseq, ncores, depth=DEPTH, trace=False):
    key = (nseq, depth)
    if key not in _PROG_CACHE:
        _PROG_CACHE[key] = build_program(nseq, depth)
    nc = _PROG_CACHE[key]
    consts = host_consts()
    f = lambda a: np.ascontiguousarray(np.asarray(a, dtype=np.float32))
    shared = {k: f(inputs[k]) for k in (
        "c_ctx", "w_ada", "b_ada", "g_norm1", "w_in", "lam_q1", "lam_k1", "lam_q2", "lam_k2", "g_subln",
        "w_conv", "w_pool", "s_pool", "w_out", "g_norm2", "w_gate_up", "w_down", "g_final")}
    shared.update(consts)
    x = f(inputs["x"])
    c = f(inputs["c"])
    ctx = f(inputs["ctx"])
    in_maps = []
    for i in range(ncores):
        m = dict(shared)
        m["x"] = x[i * nseq:(i + 1) * nseq]
        m["c"] = c[i * nseq:(i + 1) * nseq]
        m["ctx"] = ctx[i * nseq:(i + 1) * nseq]
        in_maps.append(m)
    res = run_bass_kernel_spmd(nc, in_maps, core_ids=list(range(ncores)), **({"trace": True} if trace else {}))
    out = np.concatenate([r["out"] for r in res.results], axis=0)
    return out, res


def kernel(**inputs):
    out, _ = run(inputs, BATCH // NCORES, NCORES)
    return out.astype(np.float32)
```

```python
import math
from contextlib import ExitStack

import ml_dtypes
import numpy as np

import concourse.bass as bass
import concourse.mybir as mybir
from concourse.bass_utils import run_bass_kernel_spmd

F32 = mybir.dt.float32
BF16 = mybir.dt.bfloat16
AF = mybir.ActivationFunctionType
ALU = mybir.AluOpType
AX = mybir.AxisListType

D = 1024
L = 2048
CL = 256
NT = L + CL
DEPTH = 4
BATCH = 32
NCORES = 8
KC = 8
FFN = 2816
NJ = FFN // 128
EPS = 1e-6
TT = [(0, 512), (512, 1024), (1024, 1536), (1536, 2048), (2048, 2304)]
PADW = 2336
NVEC = 73
WS_ELEMS = 2048
NSLOT = 3
ARENA_BYTES = 70144


def padcol(t):
    return 8 + t if t < L else 2072 + (t - L)


class Counter:
    LIMIT = 4000

    def __init__(self, kb, name):
        self.kb = kb
        self.name = name
        self.epoch = 0
        self.val = 0
        self.sems = [kb.new_sem(f"{name}_0")]

    def bump(self, inc):
        if self.val + inc > self.LIMIT:
            self.epoch += 1
            self.val = 0
            self.sems.append(self.kb.new_sem(f"{self.name}_{self.epoch}"))
        self.val += inc
        return self.sems[self.epoch], (self, self.epoch, self.val)

    def cur(self):
        return (self, self.epoch, self.val)


class Buf:
    __slots__ = ("w", "r", "name", "excl")

    def __init__(self, name, init_r=None, excl=False):
        self.name = name
        self.excl = excl
        self.w = None
        self.r = dict(init_r) if init_r else {}


class Eng:
    def __init__(self, kb, name, eng, is_pe=False, is_queue=False):
        self.kb = kb
        self.name = name
        self.eng = eng
        self.is_pe = is_pe
        self.is_queue = is_queue
        self.counter = None if is_queue else Counter(kb, name)
        self.seen = {}

    def wait_for(self, ev):
        if ev is None:
            return
        C, e, v = ev
        if C is self.counter:
            if self.is_pe:
                return
            if e == C.epoch and C.val - v >= 2:
                return
        s = self.seen.get(C)
        if s is not None and s >= (e, v):
            return
        if v <= 0:
            return
        self.eng.wait_ge(C.sems[e], v)
        self.seen[C] = (e, v)


class KB:
    def __init__(self, nseq, depth):
        self.nseq = nseq
        self.depth = depth
        self.nc = bass.Bass("TRN2", target_bir_lowering=False)
        self.es = ExitStack()
        self.nsem = 0

    def new_sem(self, name):
        self.nsem += 1
        return self.es.enter_context(self.nc.semaphore(f"s{self.nsem}_{name}"))

    def sb(self, name, shape, dt):
        return self.es.enter_context(self.nc.sbuf_tensor(name, shape, dt))

    def dram_in(self, name, shape, dt=F32):
        return self.nc.dram_tensor(name, list(shape), dt, kind="ExternalInput").ap()

    def _deps(self, reads, writes):
        deps = {}

        def add(ev):
            if ev is None:
                return
            C = ev[0]
            o = deps.get(C)
            if o is None or (ev[1], ev[2]) > (o[1], o[2]):
                deps[C] = ev

        for b in reads:
            add(b.w)
        for b in writes:
            add(b.w)
            for ev in b.r.values():
                add(ev)
        return deps

    def op(self, E, fn, reads=(), writes=()):
        ex = [b for b in reads if b.excl]
        if ex:
            writes = list(writes) + ex
        for ev in self._deps(reads, writes).values():
            E.wait_for(ev)
        ins = fn()
        sem, ev = E.counter.bump(1)
        ins.then_inc(sem, 1)
        for b in reads:
            b.r[E.counter] = ev
        for b in writes:
            b.w = ev
            b.r = {}
        return ev

    def dma(self, Q, out, in_, counter, reads=(), writes=(), nochain=False):
        deps = self._deps(reads, writes)
        if nochain:
            deps.pop(counter, None)
        for ev in deps.values():
            Q.wait_for(ev)
        ins = Q.eng.dma_start(out=out, in_=in_)
        sem, ev = counter.bump(16)
        ins.then_inc(sem, 16)
        for b in reads:
            b.r[counter] = ev
        for b in writes:
            b.w = ev
            b.r = {}
        return ev

    def new_phase(self):
        snap = {}
        for E in (self.PE, self.ACT, self.DVE, self.POOL):
            snap[E.counter] = E.counter.cur()
        for ev in self.arena_dma_evs:
            snap[ev[0]] = ev
        self.arena_dma_evs = []
        self.phase_snap = snap
        self.arena_off = 0

    def abuf(self, name):
        return Buf(name, init_r=self.phase_snap)

    def aalloc(self, nelem, dt):
        nb = nelem * (4 if dt == F32 else 2)
        nb = (nb + 63) // 64 * 64
        off = self.arena_off
        assert off + nb <= ARENA_BYTES, (off, nb, "arena overflow")
        self.arena_off += nb
        ap = self.arena[:, off // 2:(off + nb) // 2]
        if dt == F32:
            ap = ap.bitcast(F32)
        return ap[:, 0:nelem]

    def psA(self):
        i = self.ringA_i
        self.ringA_i = (i + 1) % 4
        return self.ps[i], self.psb[i]

    def psA2(self):
        i = self.ringA_i
        if i % 2:
            i = (i + 1) % 4
        self.ringA_i = (i + 2) % 4
        return self.psall[:, i:i + 2, :], [self.psb[i], self.psb[i + 1]]

    def psB(self):
        i = self.ringB_i
        self.ringB_i = (i + 1) % 4
        return self.ps[4 + i], self.psb[4 + i]

    def wnext(self, l, key):
        pl, pkey, n = self.plan[self.wpos]
        assert (pl, pkey) == (l, key), ((pl, pkey), (l, key), self.wpos)
        slot = self.wpos % NSLOT
        self.wpos += 1
        return self.wslot[slot], self.wslot_buf[slot]

    def wissue(self):
        i = self.wissued
        if i >= len(self.plan):
            return
        l, key, n = self.plan[i]
        slot = i % NSLOT
        ti = self.tile_index[key]
        self.dma(self.SP, self.wslot[slot][:, 0:n], self.wsc[l][ti, :, 0:n], self.wslot_cnt[slot],
                 reads=[self.wsc_buf[l]], writes=[self.wslot_buf[slot]])
        self.wissued += 1

    def wdone(self):
        self.wissue()


def layer_tiles():
    tiles = [("cgx", 0), ("cgx", 1), ("bg",), ("pool",), ("wo", 0, 0), ("wo", 0, 1)]
    for h in range(4):
        tiles += [("qk", h), ("v", h)]
    tiles += [("wo", 1, 0), ("wo", 1, 1)]
    tiles += [("gu", j) for j in range(NJ)]
    tiles += [("dn", half, jt) for half in range(2) for jt in range(6)]
    return tiles


def tile_nelems(key):
    if key[0] == "v":
        return 1024
    if key[0] == "dn" and key[2] == 5:
        return 1024
    return 2048


def ffn_token_tiles(last):
    t = [[(0, 512), (512, 1024)], [(1024, 1536), (1536, 2048)]]
    if not last:
        t.append([(2048, 2304)])
    return t


def make_plan(nseq, depth):
    plan = []
    for s in range(nseq):
        for l in range(depth):
            last = l == depth - 1
            for key in [("cgx", 0), ("cgx", 1), ("bg",), ("pool",), ("wo", 0, 0), ("wo", 0, 1)]:
                plan.append((l, key, tile_nelems(key)))
            for h in range(4):
                plan.append((l, ("qk", h), 2048))
                plan.append((l, ("v", h), 1024))
            plan.append((l, ("wo", 1, 0), 2048))
            plan.append((l, ("wo", 1, 1), 2048))
            for tok in ffn_token_tiles(last):
                for j in range(NJ):
                    plan.append((l, ("gu", j), 2048))
                for sub in tok:
                    for half in range(2):
                        for jt in range(6):
                            plan.append((l, ("dn", half, jt), tile_nelems(("dn", half, jt))))
    return plan


class StopBuild(Exception):
    pass


STOP_AT = None
CAST_ONLY = None


def ckpt(kb, name):
    if STOP_AT == name:
        for E in (kb.PE, kb.ACT, kb.DVE):
            kb.PQ.wait_for(E.counter.cur())
        raise StopBuild()


def build_program(nseq, depth):
    kb = KB(nseq, depth)
    nc = kb.nc
    with kb.es:
        try:
            _build(kb, nseq, depth)
        except StopBuild:
            pass
    return nc


def _build(kb, nseq, depth):
    nc = kb.nc
    x_d = kb.dram_in("x", [nseq, L, D])
    ctx_d = kb.dram_in("ctx", [nseq, CL, D])
    c_d = kb.dram_in("c", [nseq, D])
    cctx_d = kb.dram_in("c_ctx", [D])
    w_ada_d = kb.dram_in("w_ada", [DEPTH, D, 6 * D])
    b_ada_d = kb.dram_in("b_ada", [DEPTH, 6 * D])
    g1_d = kb.dram_in("g_norm1", [DEPTH, D])
    w_in_d = kb.dram_in("w_in", [DEPTH, D, 2560])
    lam_d = [kb.dram_in(n, [DEPTH, 64]) for n in ("lam_q1", "lam_k1", "lam_q2", "lam_k2")]
    gsub_d = kb.dram_in("g_subln", [DEPTH, 128])
    wconv_d = kb.dram_in("w_conv", [DEPTH, 3, 256])
    wpool_d = kb.dram_in("w_pool", [DEPTH, 4, 64, 64])
    spool_d = kb.dram_in("s_pool", [DEPTH, 256])
    w_out_d = kb.dram_in("w_out", [DEPTH, D, D])
    g2_d = kb.dram_in("g_norm2", [DEPTH, D])
    w_gu_d = kb.dram_in("w_gate_up", [DEPTH, D, 2 * FFN])
    w_dn_d = kb.dram_in("w_down", [DEPTH, FFN, D])
    gfin_d = kb.dram_in("g_final", [D])
    ident_d = kb.dram_in("k_ident", [128, 128])
    cos_d = kb.dram_in("k_cos", [128, L], BF16)
    sin_d = kb.dram_in("k_sin", [128, L], BF16)
    perm_d = kb.dram_in("k_perm", [128, 128], BF16)
    corr_d = kb.dram_in("k_corr", [128, 32])
    out_d = nc.dram_tensor("out", [nseq, L, D], F32, kind="ExternalOutput").ap()

    tiles = layer_tiles()
    kb.tile_index = {k: i for i, k in enumerate(tiles)}
    ntiles = len(tiles)
    kb.wsc = [nc.dram_tensor(f"wsc{l}", [ntiles, 128, WS_ELEMS], BF16, kind="Internal").ap()
              for l in range(depth)]
    kb.wsc_buf = [Buf(f"wsc{l}") for l in range(depth)]

    kb.PE = Eng(kb, "pe", nc.tensor, is_pe=True)
    kb.ACT = Eng(kb, "act", nc.scalar)
    kb.DVE = Eng(kb, "dve", nc.vector)
    kb.SP = Eng(kb, "sp", nc.sync, is_queue=True)
    kb.POOL = Eng(kb, "pool", nc.gpsimd)
    kb.PQ = kb.POOL
    PE, ACT, DVE, SP, PQ, POOL = kb.PE, kb.ACT, kb.DVE, kb.SP, kb.PQ, kb.POOL
    op, dma = kb.op, kb.dma

    xT = kb.sb("xT", [128, KC, NT], F32)
    hT = kb.sb("hT", [128, KC, NT], BF16)
    xT_b = [[Buf(f"xT{k}_{t}") for t in range(5)] for k in range(KC)]
    hT_b = [[Buf(f"hT{k}_{t}") for t in range(5)] for k in range(KC)]
    kb.arena = kb.sb("arena", [128, ARENA_BYTES // 2], BF16)
    wslots = kb.sb("wslots", [128, NSLOT, WS_ELEMS], BF16)
    kb.wslot = [wslots[:, i, :] for i in range(NSLOT)]
    kb.wslot_buf = [Buf(f"wslot{i}") for i in range(NSLOT)]
    kb.wslot_cnt = [Counter(kb, f"wsl{i}") for i in range(NSLOT)]
    cosT = kb.sb("cosT", [128, L], BF16)
    sinT = kb.sb("sinT", [128, L], BF16)
    ident = kb.sb("ident", [128, 128], F32)
    perm = kb.sb("perm", [128, 128], BF16)
    ones_bf = kb.sb("ones_bf", [128, 128], BF16)
    onesD = kb.sb("onesD", [128, 128], BF16)
    ones128 = kb.sb("ones128", [128, 128], BF16)
    ones_f32 = kb.sb("ones_f32", [128, 128], F32)
    VT = kb.sb("VT", [128, DEPTH, NVEC], F32)
    modT = kb.sb("modT", [128, DEPTH, 48, 5], F32)
    AB = kb.sb("AB", [128, DEPTH, 2, KC, 5], F32)
    neglam = kb.sb("neglam", [128, DEPTH], F32)
    gs = kb.sb("gs", [128, DEPTH], F32)
    wpbd = kb.sb("wpbd", [128, DEPTH, 2, 128], BF16)
    corr = kb.sb("corr", [128, 32], F32)
    sT = kb.sb("sT", [128, 40], F32)
    gfin = kb.sb("gfin", [128, KC], F32)
    epsT = kb.sb("epsT", [128, 1], F32)
    const_b = Buf("consts")
    psall = kb.es.enter_context(nc.psum_tensor("psall", [128, 8, 512], F32))
    kb.psall = psall
    kb.ps = [psall[:, i, :] for i in range(8)]
    kb.psb = [Buf(f"ps{i}", excl=True) for i in range(8)]
    kb.ringA_i = 0
    kb.ringB_i = 0
    kb.arena_dma_evs = []
    kb.phase_snap = {}
    kb.arena_off = 0

    c_const = Counter(kb, "cconst")
    c_lamb = Counter(kb, "clamb")
    c_rows = Counter(kb, "crows")
    c_wp = Counter(kb, "cwp")
    c_cast = [Counter(kb, f"ccast{l}") for l in range(depth)]
    c_stage = [Counter(kb, f"cstage{i}") for i in range(2)]
    c_wada = [Counter(kb, f"cwada{i}") for i in range(3)]
    c_out = [Counter(kb, f"cout{i}") for i in range(2)]

    wp_b = Buf("wpbd")
    op(DVE, lambda: nc.vector.memset(wpbd[:], 0.0), writes=[wp_b])
    for l in range(depth):
        for g in range(4):
            ci, hb = g // 2, (g % 2) * 64
            dma(PQ, wpbd[hb:hb + 64, l, ci, hb:hb + 64], wpool_d[l, g], c_wp, writes=[wp_b], nochain=True)


    def cast_block(l, dst2d, src2d, nk):
        dma(PQ, dst2d.rearrange("p (k m) -> p k m", k=nk), src2d.rearrange("(k p) m -> p k m", p=128),
            c_cast[l], writes=[kb.wsc_buf[l]], nochain=True)

    def emit_casts(l):
        for key in tiles:
            ti = kb.tile_index[key]
            dst = kb.wsc[l][ti]
            kind = key[0]
            if CAST_ONLY is not None and kind not in CAST_ONLY:
                continue
            if kind == "cgx":
                i = key[1]
                cast_block(l, dst[:, 0:1024], w_in_d[l][:, 1792 + i * 128:1792 + (i + 1) * 128], 8)
                cast_block(l, dst[:, 1024:2048], w_in_d[l][:, 2048 + i * 128:2048 + (i + 1) * 128], 8)
            elif kind == "bg":
                for i in range(2):
                    cast_block(l, dst[:, i * 1024:(i + 1) * 1024], w_in_d[l][:, 1536 + i * 128:1536 + (i + 1) * 128], 8)
            elif kind == "pool":
                for i in range(2):
                    cast_block(l, dst[:, i * 1024:(i + 1) * 1024], w_in_d[l][:, 2304 + i * 128:2304 + (i + 1) * 128], 8)
            elif kind == "wo":
                half, cog = key[1], key[2]
                rb = 512 if half == 0 else 0
                for co in range(4):
                    cc = (cog * 4 + co) * 128
                    cast_block(l, dst[:, co * 512:(co + 1) * 512], w_out_d[l][rb:rb + 512, cc:cc + 128], 4)
            elif kind == "qk":
                h = key[1]
                cast_block(l, dst[:, 0:1024], w_in_d[l][:, h * 128:(h + 1) * 128], 8)
                cast_block(l, dst[:, 1024:2048], w_in_d[l][:, 512 + h * 128:512 + (h + 1) * 128], 8)
            elif kind == "v":
                h = key[1]
                cast_block(l, dst[:, 0:1024], w_in_d[l][:, 1024 + h * 128:1024 + (h + 1) * 128], 8)
            elif kind == "gu":
                j = key[1]
                cast_block(l, dst[:, 0:1024], w_gu_d[l][:, j * 128:(j + 1) * 128], 8)
                cast_block(l, dst[:, 1024:2048], w_gu_d[l][:, FFN + j * 128:FFN + (j + 1) * 128], 8)
            elif kind == "dn":
                half, jt = key[1], key[2]
                nj = 4 if jt < 5 else 2
                dma(PQ, dst[:, 0:nj * 512].rearrange("p (j n) -> p j n", j=nj),
                    w_dn_d[l][jt * 512:jt * 512 + nj * 128, half * 512:(half + 1) * 512].rearrange(
                        "(j p) n -> p j n", p=128),
                    c_cast[l], writes=[kb.wsc_buf[l]], nochain=True)

    emit_casts(0)
    if STOP_AT == "cast":
        for l in range(depth):
            PQ.wait_for(kb.wsc_buf[l].w)
    ckpt(kb, "cast")
    kb.new_phase()
    dma(SP, ident[:], ident_d, c_const, writes=[const_b])
    dma(SP, cosT[:], cos_d, c_const, writes=[const_b])
    dma(SP, sinT[:], sin_d, c_const, writes=[const_b])
    dma(SP, perm[:], perm_d, c_const, writes=[const_b])
    dma(SP, corr[:], corr_d, c_const, writes=[const_b])
    op(DVE, lambda: nc.vector.memset(epsT[:], EPS), writes=[const_b])
    op(DVE, lambda: nc.vector.memset(ones_bf[:], 1.0), writes=[const_b])
    op(DVE, lambda: nc.vector.memset(ones_f32[:], 1.0), writes=[const_b])
    op(DVE, lambda: nc.vector.memset(onesD[:], 1.0 / D), writes=[const_b])
    op(DVE, lambda: nc.vector.memset(ones128[:], 1.0 / 128), writes=[const_b])
    rows = kb.aalloc(128, F32)
    rows_b = kb.abuf("rows")
    lamb = kb.aalloc(4 * DEPTH * 64, F32)
    lamb_b = kb.abuf("lamb")
    lamb4 = lamb.rearrange("p (a l e) -> p a l e", a=4, l=DEPTH)
    for a in range(4):
        dma(SP, lamb4[:, a, :, :].rearrange("p l e -> p (l e)"),
            lam_d[a].rearrange("l e -> (l e)").partition_broadcast(128),
            c_lamb, writes=[lamb_b])

    def load_rows_and_transpose(srcs, dst_ap, ncols):
        op(DVE, lambda: nc.vector.memset(rows, 0.0), writes=[rows_b])
        for (r0, nr, src) in srcs:
            dma(SP, rows[r0:r0 + nr, :], src, c_rows, writes=[rows_b])
        pt, ptb = kb.psA()
        op(PE, lambda: nc.tensor.transpose(pt[:, 0:128], rows, ident[:]), reads=[rows_b, const_b], writes=[ptb])
        op(DVE, lambda: nc.vector.tensor_copy(dst_ap, pt[:, 0:ncols]), reads=[ptb], writes=[const_b])

    def r128(ap1d, n):
        return ap1d.rearrange("(k p) -> k p", p=128)

    for l in range(depth):
        srcs = [(0, 8, r128(g1_d[l], 8)), (8, 8, r128(g2_d[l], 8)), (16, 48, r128(b_ada_d[l], 48)),
                (64, 2, r128(spool_d[l], 2)), (66, 1, r128(gsub_d[l], 1))]
        for k in range(3):
            srcs.append((67 + 2 * k, 2, r128(wconv_d[l, k], 2)))
        load_rows_and_transpose(srcs, VT[:, l, :], NVEC)
    srcs = [(b * 8, 8, r128(c_d[b], 8)) for b in range(nseq)]
    srcs.append((32, 8, r128(cctx_d, 8)))
    load_rows_and_transpose(srcs, sT[:], 40)
    load_rows_and_transpose([(0, 8, r128(gfin_d, 8))], gfin[:], 8)
    op(ACT, lambda: nc.scalar.activation(sT[:], sT[:], AF.Silu), reads=[const_b], writes=[const_b])
    sT3 = sT[:].rearrange("p (b k) -> p k b", k=8)

    prod = kb.aalloc(2 * DEPTH * 64, F32)
    prod_b = kb.abuf("prod")
    prod3 = prod.rearrange("p (a l e) -> p a l e", a=2, l=DEPTH)
    esum = kb.aalloc(2 * DEPTH, F32)
    esum_b = kb.abuf("esum")
    esum2 = esum.rearrange("p (a l) -> p a l", a=2)
    for a in range(2):
        op(DVE, lambda a=a: nc.vector.tensor_tensor(prod3[:, a], lamb4[:, 2 * a], lamb4[:, 2 * a + 1], ALU.mult),
           reads=[lamb_b], writes=[prod_b])
    for a in range(2):
        op(DVE, lambda a=a: nc.vector.reduce_sum(esum2[:, a, :], prod3[:, a], AX.X), reads=[prod_b], writes=[esum_b])
    op(ACT, lambda: nc.scalar.activation(esum, esum, AF.Exp), reads=[esum_b], writes=[esum_b])
    op(DVE, lambda: nc.vector.tensor_tensor(esum2[:, 0, :], esum2[:, 0, :], esum2[:, 1, :], ALU.subtract),
       reads=[esum_b], writes=[esum_b])
    for l in range(depth):
        lam_init = 0.8 - 0.6 * math.exp(-0.3 * l)
        op(DVE, lambda l=l, li=lam_init: nc.vector.tensor_scalar(
            neglam[:, l:l + 1], esum2[:, 0, l:l + 1], -1.0, -li, ALU.mult, ALU.add),
           reads=[esum_b], writes=[const_b])
        op(DVE, lambda l=l, li=lam_init: nc.vector.tensor_scalar(
            gs[:, l:l + 1], VT[:, l, 66:67], 1.0 - li, None, ALU.mult), reads=[const_b], writes=[const_b])

    sTb = kb.aalloc(40, BF16)
    sTb_b = kb.abuf("sTb")
    op(DVE, lambda: nc.vector.tensor_copy(sTb, sT[:]), reads=[const_b], writes=[sTb_b])
    sTb3 = sTb.rearrange("p (b k) -> p k b", k=8)
    NWA = 3
    wada = [kb.aalloc(KC * 512, BF16).rearrange("p (k n) -> p k n", k=KC) for _ in range(NWA)]
    wada_b = [kb.abuf(f"wada{i}") for i in range(NWA)]
    it = 0
    for l in range(depth):
        pm, pmb = kb.psB()
        for jg in range(12):
            sl = it % NWA
            it += 1
            dma(PQ, wada[sl], w_ada_d[l][:, jg * 512:(jg + 1) * 512].rearrange("(k p) n -> p k n", p=128),
                c_wada[sl], writes=[wada_b[sl]])

            def mm(l=l, jg=jg, sl=sl):
                ins = None
                for jj in range(4):
                    j = jg * 4 + jj
                    for k in range(KC):
                        ins = nc.tensor.matmul(pm[:, j * 5:(j + 1) * 5], wada[sl][:, k, jj * 128:(jj + 1) * 128],
                                               sTb3[:, k, :], start=(k == 0), stop=(k == KC - 1))
                return ins
            op(PE, mm, reads=[wada_b[sl], sTb_b], writes=[pmb])
        op(DVE, lambda l=l: nc.vector.tensor_tensor(
            modT[:, l], pm[:, 0:240].rearrange("p (j b) -> p j b", b=5),
            VT[:, l, 16:64].unsqueeze(2).to_broadcast([128, 48, 5]), ALU.add),
           reads=[pmb, const_b], writes=[const_b])
        for w_, (sc0, g0) in enumerate(((8, 0), (32, 8))):
            for k in range(KC):
                op(DVE, lambda l=l, w_=w_, k=k, sc0=sc0, g0=g0: nc.vector.tensor_scalar(
                    AB[:, l, w_, k, :], modT[:, l, sc0 + k, :], 1.0, VT[:, l, g0 + k:g0 + k + 1],
                    ALU.add, ALU.mult), reads=[const_b], writes=[const_b])
    for l in range(1, depth):
        emit_casts(l)

    ckpt(kb, "pro")
    kb.plan = make_plan(nseq, depth)
    kb.wpos = 0
    kb.wissued = 0

    def modcol(l, j, b):
        return modT[:, l, j, b:b + 1]

    def rsqrt_ps(dst, dst_b, src, src_b):
        op(ACT, lambda: nc.scalar.activation(dst, src, AF.Ln, bias=epsT[:, 0:1]), reads=[src_b, const_b], writes=[dst_b])
        op(ACT, lambda: nc.scalar.activation(dst, dst, AF.Exp, scale=-0.5), reads=[dst_b], writes=[dst_b])

    def norm_alloc():
        sq = [kb.aalloc(512, BF16) for _ in range(8)]
        sq_b = [kb.abuf(f"sq{i}") for i in range(8)]
        rstd = [kb.aalloc(512, F32) for _ in range(2)]
        rstd_b = [kb.abuf(f"rstd{i}") for i in range(2)]
        tmp = [kb.aalloc(512, F32) for _ in range(3)]
        tmp_b = [kb.abuf(f"tmp{i}") for i in range(3)]
        return sq, sq_b, rstd, rstd_b, tmp, tmp_b, [0, 0]

    def norm_tile_stages(l, s, which, t, tmps, final=False, yT=None, yT_b=None):
        sq, sq_b, rstd, rstd_b, tmp, tmp_b, cnt = tmps
        t0, t1 = TT[t]
        w = t1 - t0
        b = s if t < 4 else 4
        r = cnt[0] % 2
        cnt[0] += 1

        def s1():
            for k in range(KC):
                if k < 6:
                    op(DVE, lambda k=k: nc.vector.tensor_tensor(sq[k][:, 0:w], xT[:, k, t0:t1], xT[:, k, t0:t1], ALU.mult),
                       reads=[xT_b[k][t]], writes=[sq_b[k]])
                else:
                    op(ACT, lambda k=k: nc.scalar.activation(sq[k][:, 0:w], xT[:, k, t0:t1], AF.Square),
                       reads=[xT_b[k][t]], writes=[sq_b[k]])

        def s2():
            pa, pab = kb.psA()

            def mm():
                ins = None
                for k in range(KC):
                    ins = nc.tensor.matmul(pa[:, 0:w], onesD[:], sq[k][:, 0:w], start=(k == 0), stop=(k == KC - 1))
                return ins
            op(PE, mm, reads=sq_b + [const_b], writes=[pab])
            rsqrt_ps(rstd[r][:, 0:w], rstd_b[r], pa[:, 0:w], pab)

        def s3():
            for k in range(KC):
                i = cnt[1] % 3
                cnt[1] += 1
                op(DVE, lambda k=k, i=i: nc.vector.tensor_tensor(tmp[i][:, 0:w], xT[:, k, t0:t1], rstd[r][:, 0:w], ALU.mult),
                   reads=[xT_b[k][t], rstd_b[r]], writes=[tmp_b[i]])
                if final:
                    op(ACT, lambda k=k, i=i: nc.scalar.activation(
                        yT[:, k, 0:w], tmp[i][:, 0:w], AF.Identity, scale=gfin[:, k:k + 1]),
                       reads=[tmp_b[i], const_b], writes=[yT_b[k][0]])
                else:
                    sh = 0 if which == 0 else 24
                    op(ACT, lambda k=k, i=i: nc.scalar.activation(
                        hT[:, k, t0:t1], tmp[i][:, 0:w], AF.Identity, bias=modcol(l, sh + k, b),
                        scale=AB[:, l, which, k, b:b + 1]),
                       reads=[tmp_b[i], const_b], writes=[hT_b[k][t]])
        return [s1, s2, s3]

    def norm_phase(l, s, which, tts, final=False, yT=None, yT_b=None, tmps=None):
        tm = tmps if tmps is not None else norm_alloc()
        for t in tts:
            for st_ in norm_tile_stages(l, s, which, t, tm, final=final, yT=yT, yT_b=yT_b):
                st_()

    def proj_group(wt_ap, t, wtb, extra_reads=()):
        t0, t1 = TT[t]
        w = t1 - t0
        pa, pab = kb.psA()

        def mm():
            ins = None
            for k in range(KC):
                ins = nc.tensor.matmul(pa[:, 0:w], wt_ap[:, k * 128:(k + 1) * 128], hT[:, k, t0:t1],
                                       start=(k == 0), stop=(k == KC - 1))
            return ins
        op(PE, mm, reads=[wtb] + [hT_b[k][t] for k in range(KC)] + list(extra_reads), writes=[pab])
        return pa, pab, w

    def outproj(l, s, half, mix, mix_b, tts):
        for cog in range(2):
            wt, wtb = kb.wnext(l, ("wo", half, cog))
            for co in range(4):
                c = cog * 4 + co
                for t in tts:
                    t0, t1 = TT[t]
                    w = t1 - t0
                    b = s if t < 4 else 4
                    pa, pab = kb.psA()

                    def mm(co=co, t0=t0, t1=t1, w=w, pa=pa):
                        ins = None
                        for kc in range(4):
                            ins = nc.tensor.matmul(pa[:, 0:w], wt[:, (co * 4 + kc) * 128:(co * 4 + kc + 1) * 128],
                                                   mix[:, kc, t0:t1], start=(kc == 0), stop=(kc == 3))
                        return ins
                    op(PE, mm, reads=[wtb] + [mix_b[kc][t] for kc in range(4)], writes=[pab])
                    op(DVE, lambda c=c, t0=t0, t1=t1, w=w, pa=pa, b=b: nc.vector.scalar_tensor_tensor(
                        xT[:, c, t0:t1], pa[:, 0:w], modcol(l, 16 + c, b), xT[:, c, t0:t1], ALU.mult, ALU.add),
                       reads=[pab, const_b, xT_b[c][t]], writes=[xT_b[c][t]])
            kb.wdone()

    def outproj_tiles(l, s, half, mix, mix_b, tts, hook):
        wts = [kb.wnext(l, ("wo", half, cog)) for cog in range(2)]
        for i_, t in enumerate(tts):
            t0, t1 = TT[t]
            w = t1 - t0
            b = s if t < 4 else 4
            for cog in range(2):
                wt, wtb = wts[cog]
                for co in range(4):
                    c = cog * 4 + co
                    pa, pab = kb.psA()

                    def mm(co=co, pa=pa, wt=wt):
                        ins = None
                        for kc in range(4):
                            ins = nc.tensor.matmul(pa[:, 0:w], wt[:, (co * 4 + kc) * 128:(co * 4 + kc + 1) * 128],
                                                   mix[:, kc, t0:t1], start=(kc == 0), stop=(kc == 3))
                        return ins
                    op(PE, mm, reads=[wtb] + [mix_b[kc][t] for kc in range(4)], writes=[pab])
                    op(DVE, lambda c=c, pa=pa: nc.vector.scalar_tensor_tensor(
                        xT[:, c, t0:t1], pa[:, 0:w], modcol(l, 16 + c, b), xT[:, c, t0:t1], ALU.mult, ALU.add),
                       reads=[pab, const_b, xT_b[c][t]], writes=[xT_b[c][t]])
            hook(i_)
        kb.wdone()
        kb.wdone()

    for s in range(nseq):
        kb.new_phase()
        stage = [kb.aalloc(D, F32) for _ in range(2)]
        stage_b = [kb.abuf(f"stage{i}") for i in range(2)]
        ntm0 = norm_alloc()
        pend0 = {}
        for i in range(18):
            sl = i % 2
            src = x_d[s, i * 128:(i + 1) * 128, :] if i < 16 else ctx_d[s, (i - 16) * 128:(i - 15) * 128, :]
            dma(SP if s == 0 else PQ, stage[sl], src, c_stage[sl], writes=[stage_b[sl]])
            t = (i * 128) // 512
            for g in range(2):
                pa, pab = kb.psA()

                def tr(g=g, sl=sl, pa=pa):
                    ins = None
                    for kk in range(4):
                        k = g * 4 + kk
                        ins = nc.tensor.transpose(pa[:, kk * 128:(kk + 1) * 128], stage[sl][:, k * 128:(k + 1) * 128],
                                                  ident[:])
                    return ins
                op(PE, tr, reads=[stage_b[sl], const_b], writes=[pab])
                E = ACT if g == 0 else DVE
                dst = xT[:, g * 4:(g + 1) * 4, i * 128:(i + 1) * 128]
                srcp = pa[:, 0:512].rearrange("p (k n) -> p k n", k=4)
                if g == 0:
                    op(ACT, lambda dst=dst, srcp=srcp: nc.scalar.copy(dst, srcp), reads=[pab],
                       writes=[xT_b[k][t] for k in range(0, 4)])
                else:
                    op(DVE, lambda dst=dst, srcp=srcp: nc.vector.tensor_copy(dst, srcp), reads=[pab],
                       writes=[xT_b[k][t] for k in range(4, 8)])
            if i in (3, 7, 11, 15, 17):
                for d_, st_ in enumerate(norm_tile_stages(0, s, 0, t, ntm0)):
                    pend0.setdefault(i + d_, []).append(st_)
            for st_ in pend0.pop(i, []):
                st_()
        for i in sorted(pend0):
            for st_ in pend0[i]:
                st_()
        if s == 0:
            for _ in range(NSLOT):
                kb.wissue()

        ckpt(kb, "load")
        n1_skip = {0, 1, 2, 3, 4}
        for l in range(depth):
            last = l == depth - 1
            tts_all = [0, 1, 2, 3, 4]
            tts_x = [0, 1, 2, 3] if last else [0, 1, 2, 3, 4]
            segs = [(0, L)] if last else [(0, L), (L, NT)]

            kb.new_phase()
            norm_phase(l, s, 0, [t for t in tts_all if t not in n1_skip])
            n1_skip = set()

            ckpt(kb, "n1")
            kb.new_phase()
            cgt = [kb.aalloc(512, F32) for _ in range(2)]
            cgt_b = [kb.abuf(f"cgt{i}") for i in range(2)]
            big = [kb.aalloc(PADW, F32) for _ in range(3)]
            big_b = [kb.abuf(f"big{i}") for i in range(3)]
            pooled = kb.aalloc(NT, BF16)
            pooled_b = [kb.abuf(f"pooled{t}") for t in range(5)]
            mixcp = kb.aalloc(4 * NT, BF16).rearrange("p (c n) -> p c n", c=4)
            mixcp_b = [[kb.abuf(f"mixcp{c}_{t}") for t in range(5)] for c in range(4)]
            for i in range(3):
                op(DVE, lambda i=i: nc.vector.memset(big[i][:, 0:8], 0.0), writes=[big_b[i]])
                op(DVE, lambda i=i: nc.vector.memset(big[i][:, 2056:2072], 0.0), writes=[big_b[i]])
                op(DVE, lambda i=i: nc.vector.memset(big[i][:, 2328:2336], 0.0), writes=[big_b[i]])

            def pseg(ap, a, b_, sh=0):
                return ap[:, padcol(a) + sh:padcol(a) + sh + (b_ - a)]

            ybuf = [big[1], big[2]]
            ybuf_b = [big_b[1], big_b[2]]
            for i in range(2):
                wt, wtb = kb.wnext(l, ("cgx", i))
                for t in tts_x:
                    t0, t1 = TT[t]
                    pc, pcb, w = proj_group(wt[:, 0:1024], t, wtb)
                    px, pxb, w = proj_group(wt[:, 1024:2048], t, wtb)
                    ci = t % 2
                    op(ACT, lambda pc=pc, ci=ci, w=w: nc.scalar.copy(cgt[ci][:, 0:w], pc[:, 0:w]),
                       reads=[pcb], writes=[cgt_b[ci]])
                    op(DVE, lambda px=px, ci=ci, w=w, t0=t0, t1=t1: nc.vector.tensor_tensor(
                        pseg(big[0], t0, t1), px[:, 0:w], cgt[ci][:, 0:w], ALU.mult),
                       reads=[pxb, cgt_b[ci]], writes=[big_b[0]])
                kb.wdone()
                wc = lambda k, i=i: VT[:, l, 67 + 2 * k + i:68 + 2 * k + i]
                for (a, b_) in segs:
                    op(DVE, lambda a=a, b_=b_, i=i: nc.vector.tensor_scalar(
                        pseg(ybuf[i], a, b_), pseg(big[0], a, b_), wc(1), None, ALU.mult),
                       reads=[big_b[0], const_b], writes=[ybuf_b[i]])
                    op(DVE, lambda a=a, b_=b_, i=i: nc.vector.scalar_tensor_tensor(
                        pseg(ybuf[i], a, b_), pseg(big[0], a, b_, -1), wc(0), pseg(ybuf[i], a, b_), ALU.mult, ALU.add),
                       reads=[big_b[0], const_b, ybuf_b[i]], writes=[ybuf_b[i]])
                    op(DVE, lambda a=a, b_=b_, i=i: nc.vector.scalar_tensor_tensor(
                        pseg(ybuf[i], a, b_), pseg(big[0], a, b_, 1), wc(2), pseg(ybuf[i], a, b_), ALU.mult, ALU.add),
                       reads=[big_b[0], const_b, ybuf_b[i]], writes=[ybuf_b[i]])
            wt, wtb = kb.wnext(l, ("bg",))
            for i in range(2):
                for t in tts_x:
                    t0, t1 = TT[t]
                    pb_, pbb, w = proj_group(wt[:, i * 1024:(i + 1) * 1024], t, wtb)
                    op(DVE, lambda pb_=pb_, i=i, t0=t0, t1=t1, w=w: nc.vector.tensor_tensor(
                        mixcp[:, i, t0:t1], pb_[:, 0:w], pseg(ybuf[i], t0, t1), ALU.mult),
                       reads=[pbb, ybuf_b[i]], writes=[mixcp_b[i][t]])
            kb.wdone()
            wt, wtb = kb.wnext(l, ("pool",))
            for ci in range(2):
                P_, A_, B_ = big[0], big[1], big[2]
                Pb, Ab, Bb = big_b[0], big_b[1], big_b[2]
                for t in tts_x:
                    t0, t1 = TT[t]
                    pp, ppb, w = proj_group(wt[:, ci * 1024:(ci + 1) * 1024], t, wtb)
                    op(ACT, lambda pp=pp, t0=t0, t1=t1, w=w: nc.scalar.copy(pseg(P_, t0, t1), pp[:, 0:w]),
                       reads=[ppb], writes=[Pb])
                if ci == 1:
                    kb.wdone()

                def shift_add(O, Ob, I, Ib, sa, sb_, ext):
                    for (a, b_) in segs:
                        c0 = padcol(a) - ext
                        n = (b_ - a) + 2 * ext
                        op(DVE, lambda c0=c0, n=n: nc.vector.tensor_tensor(
                            O[:, c0:c0 + n], I[:, c0 - sa:c0 - sa + n], I[:, c0 + sb_:c0 + sb_ + n], ALU.add),
                           reads=[Ib], writes=[Ob])
                shift_add(A_, Ab, P_, Pb, 1, 0, 7)
                shift_add(B_, Bb, A_, Ab, 1, 1, 6)
                if ci == 0:
                    sel = [(0, 64, A_, Ab, 2.0), (64, 128, B_, Bb, 4.0)]
                else:
                    shift_add(A_, Ab, B_, Bb, 2, 2, 4)
                    shift_add(B_, Bb, A_, Ab, 4, 4, 0)
                    sel = [(0, 64, A_, Ab, 8.0), (64, 128, B_, Bb, 16.0)]
                for (p0, p1, S_, Sb, wwin) in sel:
                    for (a, b_) in segs:
                        c0 = padcol(a)
                        c1 = padcol(a) + (b_ - a)
                        cc = ci * 16
                        op(DVE, lambda S_=S_, p0=p0, p1=p1, c0=c0, cc=cc: nc.vector.tensor_tensor(
                            S_[p0:p1, c0:c0 + 8], S_[p0:p1, c0:c0 + 8], corr[p0:p1, cc:cc + 8], ALU.mult),
                           reads=[const_b, Sb], writes=[Sb])
                        op(DVE, lambda S_=S_, p0=p0, p1=p1, c1=c1, cc=cc: nc.vector.tensor_tensor(
                            S_[p0:p1, c1 - 8:c1], S_[p0:p1, c1 - 8:c1], corr[p0:p1, cc + 8:cc + 16], ALU.mult),
                           reads=[const_b, Sb], writes=[Sb])
                        tl = [t for t in range(5) if TT[t][0] >= a and TT[t][1] <= b_]
                        op(DVE, lambda S_=S_, p0=p0, p1=p1, c0=c0, c1=c1, a=a, b_=b_, wwin=wwin: nc.vector.scalar_tensor_tensor(
                            pooled[p0:p1, a:b_], S_[p0:p1, c0:c1], 1.0 / wwin, P_[p0:p1, c0:c1], ALU.mult, ALU.subtract),
                           reads=[Sb, Pb], writes=[pooled_b[t] for t in tl])
                for t in tts_x:
                    t0, t1 = TT[t]
                    w = t1 - t0
                    pa, pab = kb.psA()
                    op(PE, lambda pa=pa, t0=t0, t1=t1, w=w, ci=ci: nc.tensor.matmul(
                        pa[:, 0:w], wpbd[:, l, ci, :], pooled[:, t0:t1], start=True, stop=True),
                       reads=[wp_b, pooled_b[t]], writes=[pab])
                    op(ACT, lambda pa=pa, t0=t0, t1=t1, w=w, ci=ci: nc.scalar.activation(
                        mixcp[:, 2 + ci, t0:t1], pa[:, 0:w], AF.Identity, scale=VT[:, l, 64 + ci:65 + ci]),
                       reads=[pab, const_b], writes=[mixcp_b[2 + ci][t]])
            outproj(l, s, 0, mixcp, mixcp_b, tts_x)

            ckpt(kb, "cp")
            kb.new_phase()
            qT = [kb.aalloc(NT, BF16) for _ in range(2)]
            kT = [kb.aalloc(NT, BF16) for _ in range(2)]
            qT_b = [[kb.abuf(f"qT{i}_{t}") for t in range(5)] for i in range(2)]
            kT_b = [[kb.abuf(f"kT{i}_{t}") for t in range(5)] for i in range(2)]
            Vh = kb.aalloc(18 * 128, BF16).rearrange("p (i n) -> p i n", i=18)
            Vh_b = [kb.abuf(f"Vh{t}") for t in range(5)]
            Pt = [kb.aalloc(1024, BF16).rearrange("p (a n) -> p a n", a=2) for _ in range(3)]
            Pt_b = [kb.abuf(f"P{i}") for i in range(3)]
            deferred = []
            qraw = [kb.aalloc(512, BF16) for _ in range(1)]
            qraw_b = [kb.abuf(f"qraw{i}") for i in range(1)]
            t1b = kb.aalloc(512, F32)
            t1b_b = kb.abuf("t1")
            t2b = kb.aalloc(512, F32)
            t2b_b = kb.abuf("t2")
            rr = kb.aalloc(1024, F32).rearrange("p (a n) -> p a n", a=2)
            rr_b = kb.abuf("rr")
            racc = [kb.aalloc(1024, F32).rearrange("p (a n) -> p a n", a=2) for _ in range(2)]
            racc_b = [kb.abuf(f"racc{i}") for i in range(2)]
            qti = 0
            oo = [kb.aalloc(512, F32) for _ in range(2)]
            oo_b = [kb.abuf(f"oo{i}") for i in range(2)]
            sqa = kb.aalloc(512, BF16)
            sqa_b = kb.abuf("sqa")
            rsa = rr[:, 0, :]
            rsa_b = rr_b
            mixat = kb.aalloc(4 * NT, BF16).rearrange("p (c n) -> p c n", c=4)
            mixat_b = [[kb.abuf(f"mixat{c}_{t}") for t in range(5)] for c in range(4)]
            pti = 0
            qri = [0]
            bg = []

            def qk_items(h, bi):
                wt, wtb = kb.wnext(l, ("qk", h))
                items = []
                todo = [(which, t) for which in range(2) for t in tts_all if not (which == 0 and t == 4 and last)]
                for idx, (which, t) in enumerate(todo):
                    def stage1(which=which, t=t, islast=(idx == len(todo) - 1), bgbanks=False):
                        dstT, dst_b = (qT[bi], qT_b[bi]) if which == 0 else (kT[bi], kT_b[bi])
                        t0, t1 = TT[t]
                        w = t1 - t0
                        if bgbanks:
                            i0_ = kb.ringB_i
                            pq, pqb = kb.ps[4 + i0_], kb.psb[4 + i0_]
                            p2, p2b = kb.ps[4 + (i0_ + 1) % 4], kb.psb[4 + (i0_ + 1) % 4]
                        else:
                            pq, pqb = kb.psA()
                            p2, p2b = kb.psA()
                        wsl = wt[:, which * 1024:(which + 1) * 1024]

                        def mm():
                            ins = None
                            for k in range(KC):
                                ins = nc.tensor.matmul(pq[:, 0:w], wsl[:, k * 128:(k + 1) * 128], hT[:, k, t0:t1],
                                                       start=(k == 0), stop=(k == KC - 1))
                            return ins
                        op(PE, mm, reads=[wtb] + [hT_b[k][t] for k in range(KC)], writes=[pqb])
                        if islast:
                            kb.wdone()
                        if t < 4:
                            op(DVE, lambda: nc.vector.tensor_copy(qraw[0][:], pq[:]), reads=[pqb], writes=[qraw_b[0]])
                            op(DVE, lambda: nc.vector.tensor_tensor(t1b, pq[:], cosT[:, t0:t1], ALU.mult),
                               reads=[pqb, const_b], writes=[t1b_b])

                            def stage2():
                                op(PE, lambda: nc.tensor.matmul(p2[:], perm[:], qraw[0][:], start=True, stop=True),
                                   reads=[qraw_b[0], const_b], writes=[p2b])
                                op(DVE, lambda: nc.vector.tensor_tensor(t2b, p2[:], sinT[:, t0:t1], ALU.mult),
                                   reads=[p2b, const_b], writes=[t2b_b])
                                op(DVE, lambda: nc.vector.tensor_tensor(dstT[:, t0:t1], t1b, t2b, ALU.add),
                                   reads=[t1b_b, t2b_b], writes=[dst_b[t]])
                            return stage2
                        op(DVE, lambda: nc.vector.tensor_copy(dstT[:, t0:t1], pq[:, 0:w]), reads=[pqb], writes=[dst_b[t]])
                        return None
                    items.append(stage1)
                return items

            for it_ in qk_items(0, 0):
                s2_ = it_()
                if s2_:
                    s2_()
            carry = []
            for h in range(4):
                hb = h % 2
                wtv, wtvb = kb.wnext(l, ("v", h))
                for t in tts_all:
                    t0, t1 = TT[t]
                    ni = (t1 - t0) // 128
                    i0 = t0 // 128
                    pa, pab = kb.psA()

                    def mmv(pa=pa, ni=ni, i0=i0):
                        ins = None
                        for ii in range(ni):
                            tok = (i0 + ii) * 128
                            for k in range(KC):
                                ins = nc.tensor.matmul(pa[:, ii * 128:(ii + 1) * 128], hT[:, k, tok:tok + 128],
                                                       wtv[:, k * 128:(k + 1) * 128], start=(k == 0), stop=(k == KC - 1))
                        return ins
                    op(PE, mmv, reads=[wtvb] + [hT_b[k][t] for k in range(KC)], writes=[pab])
                    op(DVE, lambda pa=pa, ni=ni, i0=i0: nc.vector.tensor_copy(
                        Vh[:, i0:i0 + ni, :], pa[:, 0:ni * 128].rearrange("p (i n) -> p i n", i=ni)),
                       reads=[pab], writes=[Vh_b[t]])
                kb.wdone()

                ckpt(kb, "at_v")
                if h + 1 < 4:
                    bg.extend(qk_items(h + 1, (h + 1) % 2))
                qtiles = [0, 1, 2, 3] if last else [0, 1, 2, 3, 4]
                for qt in qtiles:
                    q0, q1 = TT[qt]
                    w = q1 - q0
                    kts = list(range(18)) if qt < 4 else [16, 17]
                    nk = len(kts)
                    sched = {}
                    for off, c_ in carry:
                        sched.setdefault(off, []).append(c_)
                    carry = []
                    if nk == 18:
                        for o1_, o2_ in ((4, 7), (9, 11), (13, 15)):
                            if bg:
                                it_ = bg.pop(0)

                                def run1(it_=it_, o2_=o2_):
                                    s2_ = it_(bgbanks=True)
                                    if s2_:
                                        sched.setdefault(o2_, []).append(s2_)
                                sched.setdefault(o1_, []).append(run1)
                    O0, O0b = kb.psB()
                    O1, O1b = kb.psB()
                    ai = qti % 2
                    qti += 1
                    acc, accb = racc[ai], racc_b[ai]
                    ntail = 2 if nk == 18 else nk
                    tailP = []

                    def score(kt, q0=q0, q1=q1, w=w):
                        sp2, spbs = kb.psA2()

                        def mm(sp2=sp2, kt=kt):
                            ins = None
                            for m in range(2):
                                ins = nc.tensor.matmul(
                                    sp2[:, m, 0:w], kT[hb][m * 64:(m + 1) * 64, kt * 128:(kt + 1) * 128],
                                    qT[hb][m * 64:(m + 1) * 64, q0:q1], start=True, stop=True)
                            return ins
                        op(PE, mm, reads=[kT_b[hb][kt // 4], qT_b[hb][qt]], writes=spbs)
                        return sp2, spbs
                    cur = score(kts[0])
                    pr2 = prbs = None
                    for n_, kt in enumerate(kts):
                        nxt = score(kts[n_ + 1]) if n_ + 1 < nk else None
                        sp2, spbs = cur
                        pi = pti % 3
                        pti += 1
                        op(ACT, lambda sp2=sp2, pi=pi, w=w: nc.scalar.activation(
                            Pt[pi][:, :, 0:w], sp2[:, :, 0:w], AF.Exp, scale=0.125), reads=spbs, writes=[Pt_b[pi]])

                        def pv(kt=kt, pi=pi, n_=n_, w=w):
                            nc.tensor.matmul(O0[:, 0:w], Vh[:, kt, :], Pt[pi][:, 0, 0:w], start=(n_ == 0), stop=(n_ == nk - 1))
                            return nc.tensor.matmul(O1[:, 0:w], Vh[:, kt, :], Pt[pi][:, 1, 0:w], start=(n_ == 0),
                                                    stop=(n_ == nk - 1))
                        op(PE, pv, reads=[Vh_b[kt // 4], Pt_b[pi]], writes=[O0b, O1b])
                        if n_ < nk - ntail:
                            if n_ == 0:
                                op(DVE, lambda pi=pi, w=w: nc.vector.tensor_copy(acc[:, :, 0:w], Pt[pi][:, :, 0:w]),
                                   reads=[Pt_b[pi]], writes=[accb])
                            else:
                                op(DVE, lambda pi=pi, w=w: nc.vector.tensor_tensor(acc[:, :, 0:w], acc[:, :, 0:w],
                                                                                  Pt[pi][:, :, 0:w], ALU.add),
                                   reads=[Pt_b[pi], accb], writes=[accb])
                        else:
                            if pr2 is None:
                                pr2, prbs = kb.psA2()
                            first = (n_ == nk - ntail)
                            lastt = (n_ == nk - 1)

                            def mmr(pr2=pr2, w=w, pi=pi, first=first, lastt=lastt, hasacc=(nk - ntail > 0)):
                                ins = None
                                for m in range(2):
                                    if first and hasacc:
                                        nc.tensor.matmul(pr2[:, m, 0:w], ones_f32[:], acc[:, m, 0:w], start=True, stop=False)
                                    ins = nc.tensor.matmul(pr2[:, m, 0:w], ones_bf[:], Pt[pi][:, m, 0:w],
                                                           start=(first and not hasacc), stop=lastt)
                                return ins
                            op(PE, mmr, reads=[accb, Pt_b[pi], const_b], writes=prbs)
                        cur = nxt
                        for c_ in sched.pop(n_, []):
                            c_()
                    for off in sorted(sched):
                        for c_ in sched[off]:
                            c_()
                    ckpt(kb, "at_s")
                    op(ACT, lambda pr2=pr2, w=w: nc.scalar.activation(rr[:, :, 0:w], pr2[:, :, 0:w], AF.Ln),
                       reads=prbs, writes=[rr_b])
                    op(ACT, lambda w=w: nc.scalar.activation(rr[:, :, 0:w], rr[:, :, 0:w], AF.Exp, scale=-1.0),
                       reads=[rr_b], writes=[rr_b])

                    def stB(w=w, O0=O0, O1=O1, O0b=O0b, O1b=O1b):
                        op(DVE, lambda: nc.vector.tensor_tensor(oo[0][:, 0:w], O0[:, 0:w], rr[:, 0, 0:w], ALU.mult),
                           reads=[O0b, rr_b], writes=[oo_b[0]])
                        op(DVE, lambda: nc.vector.scalar_tensor_tensor(
                            oo[1][:, 0:w], O1[:, 0:w], neglam[:, l:l + 1], rr[:, 1, 0:w], ALU.mult, ALU.mult),
                           reads=[O1b, rr_b, const_b], writes=[oo_b[1]])
                        op(DVE, lambda: nc.vector.tensor_tensor(oo[0][:, 0:w], oo[0][:, 0:w], oo[1][:, 0:w], ALU.add),
                           reads=[oo_b[0], oo_b[1]], writes=[oo_b[0]])
                        op(DVE, lambda: nc.vector.tensor_tensor(sqa[:, 0:w], oo[0][:, 0:w], oo[0][:, 0:w], ALU.mult),
                           reads=[oo_b[0]], writes=[sqa_b])

                    def stC(w=w, pm_=O1, pmb_=O1b):
                        op(PE, lambda: nc.tensor.matmul(pm_[:, 0:w], ones128[:], sqa[:, 0:w], start=True, stop=True),
                           reads=[sqa_b, const_b], writes=[pmb_])
                        rsqrt_ps(rsa[:, 0:w], rsa_b, pm_[:, 0:w], pmb_)

                    def stD(w=w, q0=q0, q1=q1, h=h, qt=qt):
                        op(DVE, lambda: nc.vector.scalar_tensor_tensor(
                            mixat[:, h, q0:q1], oo[0][:, 0:w], gs[:, l:l + 1], rsa[:, 0:w], ALU.mult, ALU.mult),
                           reads=[oo_b[0], rsa_b, const_b], writes=[mixat_b[h][qt]])
                    carry = [(2, stB), (6, stC), (10, stD)]
                while bg:
                    s2_ = bg.pop(0)()
                    if s2_:
                        s2_()
            for off, c_ in carry:
                c_()
            kb.phase_snap = {E.counter: E.counter.cur() for E in (PE, ACT, DVE, POOL)}
            saved_off_ = kb.arena_off
            kb.arena_off = 0
            ntm_pre = norm_alloc()
            assert kb.arena_off <= 51712
            kb.arena_off = saved_off_
            pre_st = {}
            for d_, st_ in enumerate(norm_tile_stages(l, s, 1, 0, ntm_pre)):
                pre_st.setdefault(0 + d_, []).append(st_)
            for d_, st_ in enumerate(norm_tile_stages(l, s, 1, 1, ntm_pre)):
                pre_st.setdefault(1 + d_, []).append(st_)

            def pre_hook(i_):
                for st_ in pre_st.pop(i_, []):
                    st_()
            outproj_tiles(l, s, 1, mixat, mixat_b, tts_x, pre_hook)
            for i_ in sorted(pre_st):
                for st_ in pre_st[i_]:
                    st_()

            ckpt(kb, "at")
            kb.new_phase()
            ntm = norm_alloc()
            actT = kb.aalloc(NJ * 1024, BF16).rearrange("p (j n) -> p j n", j=NJ)
            actT_b = [[kb.abuf(f"act{j}_{u}") for u in range(2)] for j in range(NJ)]
            sg = [kb.aalloc(512, F32) for _ in range(2)]
            sg_b = [kb.abuf(f"sg{i}") for i in range(2)]
            sgi = 0
            side = {}

            def add_side(ti_, j0, stages):
                for d_, st_ in enumerate(stages):
                    side.setdefault((ti_, j0 + d_), []).append(st_)
            for idx_, t in enumerate([t for t in tts_x if t >= 2]):
                add_side(0, 1 + 6 * idx_, norm_tile_stages(l, s, 1, t, ntm))
            n1_skip = set()
            if not last:
                add_side(1, 2, norm_tile_stages(l + 1, s, 0, 0, ntm))
                add_side(1, 10, norm_tile_stages(l + 1, s, 0, 1, ntm))
                add_side(2, 2, norm_tile_stages(l + 1, s, 0, 2, ntm))
                add_side(2, 10, norm_tile_stages(l + 1, s, 0, 3, ntm))
                n1_skip = {0, 1, 2, 3}
            ckpt(kb, "n2")
            for ti_, tok in enumerate(ffn_token_tiles(last)):
                for j in range(NJ):
                    wt, wtb = kb.wnext(l, ("gu", j))
                    for u, (t0, t1) in enumerate(tok):
                        t = TT.index((t0, t1))
                        w = t1 - t0
                        pg, pgb, _ = proj_group(wt[:, 0:1024], t, wtb)
                        pu, pub, _ = proj_group(wt[:, 1024:2048], t, wtb)
                        si = sgi % 2
                        sgi += 1
                        op(ACT, lambda pg=pg, si=si, w=w: nc.scalar.activation(sg[si][:, 0:w], pg[:, 0:w], AF.Silu),
                           reads=[pgb], writes=[sg_b[si]])
                        op(DVE, lambda pu=pu, si=si, w=w, j=j, u=u: nc.vector.tensor_tensor(
                            actT[:, j, u * 512:u * 512 + w], pu[:, 0:w], sg[si][:, 0:w], ALU.mult),
                           reads=[pub, sg_b[si]], writes=[actT_b[j][u]])
                    kb.wdone()
                    for c_ in side.pop((ti_, j), []):
                        c_()
                for u, (t0, t1) in enumerate(tok):
                    t = TT.index((t0, t1))
                    w = t1 - t0
                    b = s if t < 4 else 4
                    for half in range(2):
                        acc = [kb.psB() for _ in range(4)]
                        for jt in range(6):
                            wt, wtb = kb.wnext(l, ("dn", half, jt))
                            nj = 4 if jt < 5 else 2
                            for jj in range(nj):
                                j = jt * 4 + jj

                                def mmd(j=j, jj=jj, u=u, w=w, acc=acc, wt=wt):
                                    ins = None
                                    for c in range(4):
                                        ins = nc.tensor.matmul(acc[c][0][:, 0:w],
                                                               wt[:, jj * 512 + c * 128:jj * 512 + (c + 1) * 128],
                                                               actT[:, j, u * 512:u * 512 + w],
                                                               start=(j == 0), stop=(j == NJ - 1))
                                    return ins
                                op(PE, mmd, reads=[wtb, actT_b[j][u]], writes=[a[1] for a in acc])
                            kb.wdone()
                        for c in range(4):
                            co = half * 4 + c
                            op(DVE, lambda c=c, co=co, t0=t0, t1=t1, w=w, acc=acc, b=b: nc.vector.scalar_tensor_tensor(
                                xT[:, co, t0:t1], acc[c][0][:, 0:w], modcol(l, 40 + co, b), xT[:, co, t0:t1],
                                ALU.mult, ALU.add),
                               reads=[acc[c][1], const_b, xT_b[co][t]], writes=[xT_b[co][t]])

        ckpt(kb, "ffn")
        kb.new_phase()
        yTs = [kb.aalloc(KC * 512, F32).rearrange("p (k n) -> p k n", k=KC) for _ in range(2)]
        ost = [kb.aalloc(D, F32) for _ in range(2)]
        ost_b = [kb.abuf(f"ost{i}") for i in range(2)]
        ntm = norm_alloc()
        yTs_b = [[[kb.abuf(f"yT{i}_{k}")] for k in range(KC)] for i in range(2)]
        oi = 0
        nxt_st = norm_tile_stages(depth - 1, s, 0, 0, ntm, final=True, yT=yTs[0], yT_b=yTs_b[0])
        for st_ in nxt_st:
            st_()
        for t in range(4):
            yT, yT_b = yTs[t % 2], yTs_b[t % 2]
            nxt_st = (norm_tile_stages(depth - 1, s, 0, t + 1, ntm, final=True, yT=yTs[(t + 1) % 2],
                                       yT_b=yTs_b[(t + 1) % 2]) if t + 1 < 4 else [])
            for ii in range(4):
                sl = oi % 2
                oi += 1
                tok = t * 512 + ii * 128
                for g in range(2):
                    pa, pab = kb.psA()

                    def tr(g=g, ii=ii, pa=pa):
                        ins = None
                        for kk in range(4):
                            k = g * 4 + kk
                            ins = nc.tensor.transpose(pa[:, kk * 128:(kk + 1) * 128], yT[:, k, ii * 128:(ii + 1) * 128],
                                                      ident[:])
                        return ins
                    op(PE, tr, reads=[yT_b[k][0] for k in range(g * 4, g * 4 + 4)] + [const_b], writes=[pab])
                    if g == 0:
                        op(ACT, lambda pa=pa, sl=sl: nc.scalar.copy(ost[sl][:, 0:512], pa[:]), reads=[pab], writes=[ost_b[sl]])
                    else:
                        op(DVE, lambda pa=pa, sl=sl: nc.vector.tensor_copy(ost[sl][:, 512:1024], pa[:]), reads=[pab],
                           writes=[ost_b[sl]])
                if ii < len(nxt_st):
                    nxt_st[ii]()
                ev = dma(PQ, out_d[s, tok:tok + 128, :], ost[sl], c_out[sl], reads=[ost_b[sl]])
                kb.arena_dma_evs.append(ev)
                kb.final_out_evs = getattr(kb, "final_out_evs", {})
                kb.final_out_evs[c_out[sl]] = ev

    for ev in kb.final_out_evs.values():
        PQ.wait_for(ev)
    assert kb.wpos == len(kb.plan), (kb.wpos, len(kb.plan))


def host_consts():
    i = np.arange(128)
    i64 = i % 64
    a = i64 // 32
    half = (i64 % 32) // 16
    f = i64 % 16
    t = np.arange(L)
    row = (t // 64).astype(np.float32)
    col = (t % 64).astype(np.float32)
    inv = (1.0 / (10000.0 ** (np.arange(16, dtype=np.float32) / 16))).astype(np.float32)
    pos = np.where(a[:, None] == 0, row[None, :], col[None, :]).astype(np.float32)
    ang = (pos * inv[f][:, None]).astype(np.float32)
    cos = np.cos(ang).astype(np.float32)
    sin = np.sin(ang).astype(np.float32)
    sin_s = np.where(half[:, None] == 0, -sin, sin).astype(np.float32)
    perm = np.zeros((128, 128), np.float32)
    perm[i ^ 16, i] = 1.0
    corr = np.ones((128, 32), np.float32)
    for ci in range(2):
        for p in range(128):
            w = (2, 4, 8, 16)[ci * 2 + (p // 64)]
            for c in range(8):
                lo = max(c - w // 2, 0)
                hi = c + w - 1 - w // 2
                corr[p, ci * 16 + c] = w / float(hi - lo + 1)
                d = 7 - c
                hi_off = min(w - 1 - w // 2, d)
                cnt = w // 2 + hi_off + 1
                corr[p, ci * 16 + 8 + c] = w / float(cnt)
    return {
        "k_ident": np.eye(128, dtype=np.float32),
        "k_cos": cos.astype(ml_dtypes.bfloat16),
        "k_sin": sin_s.astype(ml_dtypes.bfloat16),
        "k_perm": perm.astype(ml_dtypes.bfloat16),
        "k_corr": corr,
    }


_PROG_CACHE = {}


def run(inputs, nseq, ncores, depth=DEPTH, trace=False):
    key = (nseq, depth)
    if key not in _PROG_CACHE:
        _PROG_CACHE[key] = build_program(nseq, depth)
    nc = _PROG_CACHE[key]
    consts = host_consts()
    f = lambda a: np.ascontiguousarray(np.asarray(a, dtype=np.float32))
    shared = {k: f(inputs[k]) for k in (
        "c_ctx", "w_ada", "b_ada", "g_norm1", "w_in", "lam_q1", "lam_k1", "lam_q2", "lam_k2", "g_subln",
        "w_conv", "w_pool", "s_pool", "w_out", "g_norm2", "w_gate_up", "w_down", "g_final")}
    shared.update(consts)
    x = f(inputs["x"])
    c = f(inputs["c"])
    ctx = f(inputs["ctx"])
    in_maps = []
    for i in range(ncores):
        m = dict(shared)
        m["x"] = x[i * nseq:(i + 1) * nseq]
        m["c"] = c[i * nseq:(i + 1) * nseq]
        m["ctx"] = ctx[i * nseq:(i + 1) * nseq]
        in_maps.append(m)
    res = run_bass_kernel_spmd(nc, in_maps, core_ids=list(range(ncores)), **({"trace": True} if trace else {}))
    out = np.concatenate([r["out"] for r in res.results], axis=0)
    return out, res


def kernel(**inputs):
    out, _ = run(inputs, BATCH // NCORES, NCORES)
    return out.astype(np.float32)
```
